# Optimizing a Trainium2 kernel written in Bass

```python
import math
import jax, jax.numpy as jnp
from jax import lax
import numpy as np

D_MODEL = 1024
BATCH = 8
SEQ = 8192
DEPTH = 1
DEC_BATCH = 4
DEC_SEQ = 8192
PAST_LEN = 128

D_HYENA = 512
D_CONV = 512
HYENA_SHORT = 3
CONF_KERNEL = 31
FILTER_EMB = 33
FILTER_BANDS = (FILTER_EMB - 1) // 2
FILTER_HIDDEN = 64
DECAY_TARGET = 1e-2
FAST_DECAY_PCT = 0.3
SLOW_DECAY_PCT = 1.5
D_FF = 2816
N_IN = 3 * D_HYENA + 2 * D_CONV + 2 * D_MODEL
EPS = 1e-6

kernel_name = "hyena_conformer_gated_hybrid_encoder"


def rms_norm(x, g):
    xf = x.astype(jnp.float32)
    y = xf * lax.rsqrt(jnp.mean(xf * xf, axis=-1, keepdims=True) + EPS)
    return (y * g.astype(jnp.float32)).astype(x.dtype)


def layer_norm(x, g, b):
    xf = x.astype(jnp.float32)
    mu = jnp.mean(xf, axis=-1, keepdims=True)
    xc = xf - mu
    y = xc * lax.rsqrt(jnp.mean(xc * xc, axis=-1, keepdims=True) + EPS)
    return (y * g.astype(jnp.float32) + b.astype(jnp.float32)).astype(x.dtype)


def swiglu(x, w_gate, w_up, w_down):
    return (jax.nn.silu(x @ w_gate) * (x @ w_up)) @ w_down


def depthwise_conv(x, w, b):
    K, C = w.shape
    y = lax.conv_general_dilated(
        x, w[:, None, :].astype(x.dtype), window_strides=(1,),
        padding=[(K // 2, K // 2)], dimension_numbers=("NWC", "WIO", "NWC"),
        feature_group_count=C)
    return y + b.astype(x.dtype)


def hyena_filter_spectrum(L, w1, b1, freq1, w2, b2, freq2, w3):
    f32 = jnp.float32
    t = jnp.linspace(0.0, 1.0, L, dtype=f32)[:, None]
    w = (2.0 * math.pi / L) * jnp.arange(L, dtype=f32)[:, None]
    bands = jnp.linspace(1e-4, FILTER_BANDS - 1, FILTER_BANDS, dtype=f32)
    ang = w * bands[None, :]
    z = jnp.concatenate([t, jnp.cos(ang), -jnp.sin(ang)], axis=-1)
    h = jnp.sin(freq1.astype(f32) * (z @ w1.astype(f32) + b1.astype(f32)))
    h = jnp.sin(freq2.astype(f32) * (h @ w2.astype(f32) + b2.astype(f32)))
    h = (h @ w3.astype(f32)).reshape(L, 2, D_HYENA)
    max_decay = math.log(DECAY_TARGET) / FAST_DECAY_PCT
    min_decay = math.log(DECAY_TARGET) / SLOW_DECAY_PCT
    deltas = jnp.linspace(min_decay, max_decay, D_HYENA, dtype=f32)
    h = h * jnp.exp(-t * jnp.abs(deltas))[:, None, :]
    fwd, bwd = h[:, 0], h[:, 1]
    k = jnp.concatenate([fwd[:1] + bwd[:1], fwd[1:], jnp.zeros((1, D_HYENA), f32), bwd[:0:-1]], axis=0)
    k = k * lax.rsqrt(jnp.sum(k * k, axis=0, keepdims=True) + EPS)
    return jnp.fft.rfft(k, axis=0)


def long_conv(v, k_f, bias):
    L = v.shape[1]
    vf32 = v.astype(jnp.float32)
    vf = jnp.fft.rfft(vf32, n=2 * L, axis=1)
    y = jnp.fft.irfft(vf * k_f[None], n=2 * L, axis=1)[:, :L]
    return (y + vf32 * bias.astype(jnp.float32)).astype(v.dtype)


def token_mixer(u, k_f, w_in, hy_short_w, hy_short_b, hy_bias, hy_w_out,
                cv_dw_w, cv_dw_b, cv_ln_g, cv_ln_b, cv_w_out, w_out):
    proj = u @ w_in
    o1 = 3 * D_HYENA
    o2 = o1 + 2 * D_CONV
    hy = depthwise_conv(proj[..., :o1], hy_short_w, hy_short_b)
    cv = proj[..., o1:o2]
    g_a = proj[..., o2:o2 + D_MODEL]
    g_b = proj[..., o2 + D_MODEL:]
    x0 = hy[..., :D_HYENA]
    x1 = hy[..., D_HYENA:2 * D_HYENA]
    v = hy[..., 2 * D_HYENA:]
    v = long_conv(v * x1, k_f, hy_bias)
    y_a = (v * x0) @ hy_w_out
    c = cv[..., :D_CONV] * jax.nn.sigmoid(cv[..., D_CONV:])
    c = depthwise_conv(c, cv_dw_w, cv_dw_b)
    c = jax.nn.silu(layer_norm(c, cv_ln_g, cv_ln_b))
    y_b = c @ cv_w_out
    merged = jax.nn.sigmoid(g_a) * y_a + jax.nn.sigmoid(g_b) * y_b
    return merged @ w_out


def encoder_layer(x, ffn1_norm_pre, ffn1_w_gate, ffn1_w_up, ffn1_w_down, ffn1_norm_post,
                  mix_norm_pre, w_in, hy_short_w, hy_short_b,
                  hy_filt_w1, hy_filt_b1, hy_filt_freq1, hy_filt_w2, hy_filt_b2, hy_filt_freq2, hy_filt_w3,
                  hy_bias, hy_w_out, cv_dw_w, cv_dw_b, cv_ln_g, cv_ln_b, cv_w_out, w_out, mix_norm_post,
                  ffn2_norm_pre, ffn2_w_gate, ffn2_w_up, ffn2_w_down, ffn2_norm_post):
    L = x.shape[1]
    k_f = hyena_filter_spectrum(L, hy_filt_w1, hy_filt_b1, hy_filt_freq1,
                                hy_filt_w2, hy_filt_b2, hy_filt_freq2, hy_filt_w3)
    x = x + 0.5 * rms_norm(swiglu(rms_norm(x, ffn1_norm_pre), ffn1_w_gate, ffn1_w_up, ffn1_w_down), ffn1_norm_post)
    m = token_mixer(rms_norm(x, mix_norm_pre), k_f, w_in, hy_short_w, hy_short_b, hy_bias, hy_w_out,
                    cv_dw_w, cv_dw_b, cv_ln_g, cv_ln_b, cv_w_out, w_out)
    x = x + rms_norm(m, mix_norm_post)
    x = x + 0.5 * rms_norm(swiglu(rms_norm(x, ffn2_norm_pre), ffn2_w_gate, ffn2_w_up, ffn2_w_down), ffn2_norm_post)
    return x


def setup_inputs(seed: int = 0) -> dict:
    key = jax.random.key(seed)
    ks = iter(jax.random.split(key, 48))
    f32 = jnp.float32

    def nrm(shape, scale):
        return jax.random.normal(next(ks), shape, f32) * scale

    def gain(n):
        return 1.0 + 0.1 * jax.random.normal(next(ks), (DEPTH, n), f32)

    d = {}
    d["x_prompt"] = nrm((BATCH, SEQ, D_MODEL), 1.0)
    d["x_sample"] = nrm((DEC_BATCH, DEC_SEQ, D_MODEL), 1.0)
    d["ffn1_norm_pre"] = gain(D_MODEL)
    d["ffn1_w_gate"] = nrm((DEPTH, D_MODEL, D_FF), D_MODEL ** -0.5)
    d["ffn1_w_up"] = nrm((DEPTH, D_MODEL, D_FF), D_MODEL ** -0.5)
    d["ffn1_w_down"] = nrm((DEPTH, D_FF, D_MODEL), D_FF ** -0.5)
    d["ffn1_norm_post"] = gain(D_MODEL)
    d["mix_norm_pre"] = gain(D_MODEL)
    d["w_in"] = nrm((DEPTH, D_MODEL, N_IN), D_MODEL ** -0.5)
    d["hy_short_w"] = nrm((DEPTH, HYENA_SHORT, 3 * D_HYENA), HYENA_SHORT ** -0.5)
    d["hy_short_b"] = nrm((DEPTH, 3 * D_HYENA), 0.02)
    d["hy_filt_w1"] = nrm((DEPTH, FILTER_EMB, FILTER_HIDDEN), FILTER_EMB ** -0.5)
    d["hy_filt_b1"] = nrm((DEPTH, FILTER_HIDDEN), 0.1)
    d["hy_filt_freq1"] = gain(FILTER_HIDDEN)
    d["hy_filt_w2"] = nrm((DEPTH, FILTER_HIDDEN, FILTER_HIDDEN), FILTER_HIDDEN ** -0.5)
    d["hy_filt_b2"] = nrm((DEPTH, FILTER_HIDDEN), 0.1)
    d["hy_filt_freq2"] = gain(FILTER_HIDDEN)
    d["hy_filt_w3"] = nrm((DEPTH, FILTER_HIDDEN, 2 * D_HYENA), FILTER_HIDDEN ** -0.5)
    d["hy_bias"] = nrm((DEPTH, D_HYENA), 0.5)
    d["hy_w_out"] = nrm((DEPTH, D_HYENA, D_MODEL), D_HYENA ** -0.5)
    d["cv_dw_w"] = nrm((DEPTH, CONF_KERNEL, D_CONV), CONF_KERNEL ** -0.5)
    d["cv_dw_b"] = nrm((DEPTH, D_CONV), 0.02)
    d["cv_ln_g"] = gain(D_CONV)
    d["cv_ln_b"] = nrm((DEPTH, D_CONV), 0.02)
    d["cv_w_out"] = nrm((DEPTH, D_CONV, D_MODEL), D_CONV ** -0.5)
    d["w_out"] = nrm((DEPTH, D_MODEL, D_MODEL), D_MODEL ** -0.5)
    d["mix_norm_post"] = gain(D_MODEL)
    d["ffn2_norm_pre"] = gain(D_MODEL)
    d["ffn2_w_gate"] = nrm((DEPTH, D_MODEL, D_FF), D_MODEL ** -0.5)
    d["ffn2_w_up"] = nrm((DEPTH, D_MODEL, D_FF), D_MODEL ** -0.5)
    d["ffn2_w_down"] = nrm((DEPTH, D_FF, D_MODEL), D_FF ** -0.5)
    d["ffn2_norm_post"] = gain(D_MODEL)
    return d


def reference(x_prompt, x_sample, ffn1_norm_pre, ffn1_w_gate, ffn1_w_up, ffn1_w_down, ffn1_norm_post,
              mix_norm_pre, w_in, hy_short_w, hy_short_b,
              hy_filt_w1, hy_filt_b1, hy_filt_freq1, hy_filt_w2, hy_filt_b2, hy_filt_freq2, hy_filt_w3,
              hy_bias, hy_w_out, cv_dw_w, cv_dw_b, cv_ln_g, cv_ln_b, cv_w_out, w_out, mix_norm_post,
              ffn2_norm_pre, ffn2_w_gate, ffn2_w_up, ffn2_w_down, ffn2_norm_post):
    y_prompt = x_prompt
    y_sample = x_sample
    for l in range(DEPTH):
        params = (ffn1_norm_pre[l], ffn1_w_gate[l], ffn1_w_up[l], ffn1_w_down[l], ffn1_norm_post[l],
                  mix_norm_pre[l], w_in[l], hy_short_w[l], hy_short_b[l],
                  hy_filt_w1[l], hy_filt_b1[l], hy_filt_freq1[l], hy_filt_w2[l], hy_filt_b2[l],
                  hy_filt_freq2[l], hy_filt_w3[l],
                  hy_bias[l], hy_w_out[l], cv_dw_w[l], cv_dw_b[l], cv_ln_g[l], cv_ln_b[l], cv_w_out[l],
                  w_out[l], mix_norm_post[l],
                  ffn2_norm_pre[l], ffn2_w_gate[l], ffn2_w_up[l], ffn2_w_down[l], ffn2_norm_post[l])
        y_prompt = encoder_layer(y_prompt, *params)
        y_sample = encoder_layer(y_sample, *params)
    return (y_prompt, y_sample)
```

```python
import contextlib
import numpy as np
import concourse.bass as bass
import concourse.mybir as mybir

F32 = mybir.dt.float32
BF16 = mybir.dt.bfloat16
AF = mybir.ActivationFunctionType
ALU = mybir.AluOpType

ENGS = ["pe", "act", "dve", "pool", "sp"]


class Buf:
    __slots__ = ("name", "lw", "rd", "dsem", "dcnt", "const", "excl")

    def __init__(self, name, const=False, excl=False):
        self.excl = excl
        self.name = name
        self.lw = None
        self.rd = []
        self.dsem = None
        self.dcnt = 0
        self.const = const


class Op:
    __slots__ = ("eng", "fns", "waits", "sig", "ordinal", "dma", "ev")

    def __init__(self, eng, fns, dma):
        self.eng = eng
        self.fns = fns
        self.waits = []
        self.sig = False
        self.ordinal = None
        self.dma = dma
        self.ev = None


class Sched:
    def __init__(self, nc, stack, n_dma_sems=48):
        self.nc = nc
        self.esem = {e: stack.enter_context(nc.semaphore("es_" + e)) for e in ENGS}
        self.bar = stack.enter_context(nc.semaphore("barrier"))
        self.dsems = [stack.enter_context(nc.semaphore("ds%d" % i)) for i in range(n_dma_sems)]
        self.dsem_cnt = [0] * n_dma_sems
        self.dsem_free = list(range(n_dma_sems))
        self.ecount = {e: 0 for e in ENGS}
        self.barcount = 0
        self.ops = {e: [] for e in ENGS}
        self.bufs = []

    def buf(self, name, const=False, excl=False):
        b = Buf(name, const, excl)
        self.bufs.append(b)
        return b

    def _alloc_dsem(self, b):
        if b.dsem is None:
            b.dsem = self.dsem_free.pop(0)
        return b.dsem

    def _deps(self, reads, writes):
        deps = []
        for b in reads:
            if b.lw is not None:
                deps.append(b.lw)
        for b in writes:
            if b.lw is not None:
                deps.append(b.lw)
            deps.extend(b.rd)
        return deps

    def _finish(self, op, ev, reads, writes):
        for b in writes:
            b.lw = ev
            b.rd = []
        for b in reads:
            if b in writes or b.const:
                continue
            b.rd.append(ev)

    def op(self, eng, fns, reads=(), writes=()):
        if callable(fns):
            fns = [fns]
        ex = [b for b in reads if b.excl]
        if ex:
            reads = [b for b in reads if not b.excl]
            writes = list(writes) + ex
        o = Op(eng, fns, False)
        for d in self._deps(reads, writes):
            self._add_wait(o, d)
        ev = ("c", o)
        o.ev = ev
        self.ops[eng].append(o)
        self._finish(o, ev, reads, writes)
        return o

    def dma(self, queue, fns, sbuf, reads=(), writes=()):
        if callable(fns):
            fns = [fns]
        o = Op(queue, fns, True)
        for d in self._deps(reads, writes):
            self._add_wait(o, d)
        si = self._alloc_dsem(sbuf)
        self.dsem_cnt[si] += 16 * len(fns)
        ev = ("d", si, self.dsem_cnt[si])
        o.ev = ev
        self.ops[queue].append(o)
        self._finish(o, ev, reads, writes)
        return o

    def _add_wait(self, o, d):
        if d[0] == "c":
            d[1].sig = True
        o.waits.append(d)

    def flush(self, final=False, keep=False):
        nc = self.nc
        last_ops = {}
        for e in ENGS:
            for o in reversed(self.ops[e]):
                if not o.dma:
                    o.sig = True
                    last_ops[e] = o
                    break
        for e in ENGS:
            for o in self.ops[e]:
                if (not o.dma) and o.sig:
                    self.ecount[e] += 1
                    o.ordinal = self.ecount[e]
        self.barcount += 1
        barval = self.barcount
        dma_final = [(i, c) for i, c in enumerate(self.dsem_cnt) if c > 0]
        ecount = dict(self.ecount)
        known0 = getattr(self, "_known", {e: {x: 0 for x in ENGS} for e in ENGS})
        kd0 = getattr(self, "_kd", {e: [0] * len(self.dsems) for e in ENGS})

        def emit(engobj, e):
            known = known0[e]
            kd = kd0[e]
            for o in self.ops[e]:
                for d in o.waits:
                    if d[0] == "c":
                        po = d[1]
                        if po.ordinal > known[po.eng]:
                            engobj.wait_ge(self.esem[po.eng], po.ordinal)
                            known[po.eng] = po.ordinal
                    else:
                        _, si, cnt = d
                        if cnt > kd[si]:
                            engobj.wait_ge(self.dsems[si], cnt)
                            kd[si] = cnt
                n = len(o.fns)
                for i, fn in enumerate(o.fns):
                    ins = fn(engobj)
                    if o.dma:
                        ins.then_inc(self.dsems[o.ev[1]], 16)
                    elif o.sig and i == n - 1:
                        ins.then_inc(self.esem[e], 1)
            if e == "sp":
                for x in ENGS:
                    if ecount[x] > known[x]:
                        engobj.wait_ge(self.esem[x], ecount[x])
                        known[x] = ecount[x]
                for si, c in dma_final:
                    if c > kd[si]:
                        engobj.wait_ge(self.dsems[si], c)
                        kd[si] = c
                engobj.sem_inc(self.bar, 1)
            engobj.wait_ge(self.bar, barval)
            for x in ENGS:
                known[x] = ecount[x]
            for si, c in dma_final:
                kd[si] = c

        with nc.Block() as block:
            block.tensor(lambda eo: emit(eo, "pe"))
            block.scalar(lambda eo: emit(eo, "act"))
            block.vector(lambda eo: emit(eo, "dve"))
            block.gpsimd(lambda eo: emit(eo, "pool"))
            block.sync(lambda eo: emit(eo, "sp"))
        self._known = known0
        self._kd = kd0
        n_ops = {e: len(self.ops[e]) for e in ENGS}
        self.ops = {e: [] for e in ENGS}
        for b in self.bufs:
            b.lw = None
            b.rd = []
            if b.dsem is not None:
                self.dsem_free.append(b.dsem)
                b.dsem = None
        if not keep:
            self.bufs = []
        return n_ops

import contextlib
import numpy as np

D = 1024
DFF = 2816
NKC = D // 128
NJ = DFF // 128
EPS = 1e-6


_uid = [0]


def uname(name):
    _uid[0] += 1
    return "%s_%d" % (name, _uid[0])


def load_bcast_gain(nc, S, st, name, g_dram, scale):
    n = g_dram.shape[-1]
    t = st.enter_context(nc.sbuf_tensor(uname(name), [128, n], F32))
    b = S.buf(name)
    S.dma("sp", lambda e: e.dma_start(out=t[:], in_=g_dram.partition_broadcast(128)), b, writes=[b])
    if scale != 1.0:
        S.op("act", lambda e: e.mul(t[:], t[:], float(scale)), reads=[b], writes=[b])
    return t, b


def make_ident(nc, S, st, dtype=BF16, name="ident"):
    idf = st.enter_context(nc.sbuf_tensor(uname(name + "_f"), [128, 128], F32))
    idt = st.enter_context(nc.sbuf_tensor(uname(name), [128, 128], dtype))
    b = S.buf(name)

    def f1(e):
        return e.memset(idf[:], 1.0)

    def f2(e):
        return e.affine_select(out=idf[:], in_=idf[:], pattern=[[-1, 128]], compare_op=ALU.is_equal,
                               fill=0.0, base=0, channel_multiplier=1)

    def f3(e):
        return e.tensor_copy(out=idt[:], in_=idf[:])
    S.op("pool", f1, writes=[b])
    S.op("pool", f2, reads=[b], writes=[b])
    S.op("pool", f3, reads=[b], writes=[b])
    return idt, b

import contextlib
import numpy as np

L = 8192
PAD = 32
LP = L + 2 * PAD
DH = 512
NHY = 1536


def mk_alloc(nc, st):
    sb = lambda name, shape, dt: st.enter_context(nc.sbuf_tensor(uname(name), shape, dt))
    ps = lambda name, shape, dt: st.enter_context(nc.psum_tensor(uname(name), shape, dt))
    return sb, ps


def load_rows_T(nc, S, st, name, mat_dram, R, C, identf, bidentf, psum_t, bpsum_t):
    n = C // 128
    rows = st.enter_context(nc.sbuf_tensor(uname(name + "_rows"), [R, C], F32))
    brows = S.buf(name + "_rows")
    S.dma("sp", lambda e: e.dma_start(out=rows[:], in_=mat_dram), brows, writes=[brows])
    t = st.enter_context(nc.sbuf_tensor(uname(name), [128, n, R], F32))
    b = S.buf(name, True)
    for j in range(n):
        S.op("pe", lambda e, j=j: e.transpose(out=psum_t[:, 0:R], in_=rows[:, j * 128:(j + 1) * 128], identity=identf[0:R, 0:R]),
             reads=[brows, bidentf], writes=[bpsum_t])
        S.op("dve", lambda e, j=j: e.tensor_copy(out=t[:, j, :], in_=psum_t[:, 0:R]), reads=[bpsum_t], writes=[b])
    return t, b


def scale_weight_rows(nc, S, st, W, bW, g_dram, factor, width):
    gc = st.enter_context(nc.sbuf_tensor(uname("gcol"), [128, NKC], F32))
    bgc = S.buf("gcol")
    S.dma("sp", lambda e: e.dma_start(out=gc[:], in_=g_dram.rearrange("(kc p) -> p kc", p=128), allow_slow_non_contiguous=True), bgc, writes=[bgc])
    S.op("act", lambda e: e.mul(gc[:], gc[:], float(factor)), reads=[bgc], writes=[bgc])
    for kc in range(NKC):
        if kc % 2 == 0:
            S.op("dve", lambda e, kc=kc: e.tensor_scalar(out=W[:, kc, 0:width], in0=W[:, kc, 0:width], scalar1=gc[:, kc:kc + 1], scalar2=None, op0=ALU.mult),
                 reads=[bgc, bW], writes=[bW])
        else:
            S.op("act", lambda e, kc=kc: e.activation(out=W[:, kc, 0:width], in_=W[:, kc, 0:width], func=AF.Copy, scale=gc[:, kc:kc + 1]),
                 reads=[bgc, bW], writes=[bW])


class NormFront:
    def __init__(self, nc, S, st, T, n_uT=2):
        sb, ps = mk_alloc(nc, st)
        self.nc, self.S, self.T, self.NS = nc, S, T, T // 128
        NS = self.NS
        self.ident, self.bid = make_ident(nc, S, st, dtype=F32, name="identn")
        self.bid.const = True
        self.cst = sb("cst", [128, 2], F32)
        self.bcst = S.buf("cst", True)
        S.op("pool", [lambda e: e.memset(self.cst[:, 0:1], float(D * EPS)), lambda e: e.memset(self.cst[:, 1:2], -0.5)], writes=[self.bcst])
        self.junk = sb("junk", [128, D], BF16)
        self.bjunk = S.buf("junk")
        ncol = 6 * NS
        self.ssq = sb("ssq", [128, ncol], F32)
        self.rstd = sb("rstd", [128, ncol], F32)
        self.bssq = [S.buf("ssq%d" % i) for i in range(ncol)]
        self.brstd = [S.buf("rstd%d" % i) for i in range(ncol)]
        self.xb = [sb("xb%d" % i, [128, D], BF16) for i in range(NS)]
        self.bxb = [S.buf("xb%d" % i) for i in range(NS)]
        self.Rd = [sb("Rd%d" % i, [128, 128], BF16) for i in range(NS)]
        self.bRd = [S.buf("Rd%d" % i) for i in range(NS)]
        self.n_uT = n_uT
        self.uT = [sb("uT%d" % i, [128, NKC, T], BF16) for i in range(n_uT)]
        self.buT = [S.buf("uT%d" % i) for i in range(n_uT)]
        self.psT = [ps("psT%d" % i, [128, NKC // 2, 128], F32) for i in range(2)]
        self.bpsT = [S.buf("psT%d" % i, excl=True) for i in range(2)]

    def col(self, par, post, s):
        return (par % 2) * 3 * self.NS + post * self.NS + s

    def rstd_op(self, col):
        S = self.S
        ssq, rstd, cst = self.ssq, self.rstd, self.cst
        S.op("pool", lambda e: e.tensor_tensor(out=rstd[:, col:col + 1], in0=ssq[:, col:col + 1], in1=cst[:, 0:1], op=ALU.add),
             reads=[self.bssq[col], self.bcst], writes=[self.brstd[col]])
        S.op("pool", lambda e: e.tensor_tensor(out=rstd[:, col:col + 1], in0=rstd[:, col:col + 1], in1=cst[:, 1:2], op=ALU.pow),
             reads=[self.brstd[col], self.bcst], writes=[self.brstd[col]])

    def front_a(self, X, bX, par):
        S = self.S
        for s in range(self.NS):
            c = self.col(par, 0, s)
            S.op("act", lambda e, s=s, c=c: e.activation(out=self.junk[:], in_=X[:, s, :], func=AF.Square, accum_out=self.ssq[:, c:c + 1]),
                 reads=[bX], writes=[self.bjunk, self.bssq[c]])
            self.rstd_op(c)
            S.op("dve", lambda e, s=s, c=c: e.tensor_scalar(out=self.Rd[s][:], in0=self.ident[:], scalar1=self.rstd[:, c:c + 1], scalar2=None, op0=ALU.mult),
                 reads=[self.bid, self.brstd[c]], writes=[self.bRd[s]])
            S.op("dve", lambda e, s=s: e.tensor_copy(out=self.xb[s][:], in_=X[:, s, :]), reads=[bX], writes=[self.bxb[s]])

    def front_b_s(self, par, s):
        S = self.S
        uT, buT = self.uT[par % self.n_uT], self.buT[par % self.n_uT]
        H = NKC // 2
        for hf in range(2):
            P, bP = self.psT[hf], self.bpsT[hf]
            S.op("pe", [(lambda e, s=s, kc=kc, P=P, hf=hf: e.matmul(P[:, kc - hf * H, :], lhsT=self.xb[s][:, kc * 128:(kc + 1) * 128], rhs=self.Rd[s][:], start=True, stop=True))
                        for kc in range(hf * H, (hf + 1) * H)], reads=[self.bxb[s], self.bRd[s]], writes=[bP])
            S.op("act", lambda e, s=s, uT=uT, P=P, hf=hf: e.copy(out=uT[:, hf * H:(hf + 1) * H, s * 128:(s + 1) * 128], in_=P[:]),
                 reads=[bP], writes=[buT])
        return uT, buT

    def front_b(self, par):
        for s in range(self.NS):
            r = self.front_b_s(par, s)
        return r

    def post_half(self, psDh, bpsDh, gpost, bgpost, tmp, btmp, s, par, h):
        S = self.S
        q = self.col(par, 1 + h, s)
        S.op("act", lambda e: e.activation(out=self.junk[:, 0:512], in_=psDh[:], func=AF.Square, accum_out=self.ssq[:, q:q + 1]),
             reads=[bpsDh], writes=[self.bjunk, self.bssq[q]])
        S.op("dve", lambda e: e.tensor_tensor(out=tmp[:, h * 512:(h + 1) * 512], in0=psDh[:], in1=gpost[:, h * 512:(h + 1) * 512], op=ALU.mult),
             reads=[bpsDh, bgpost], writes=[btmp[h]])

    def post_fin(self, tmp, btmp, X, bX, s, par):
        S = self.S
        q0, q1 = self.col(par, 1, s), self.col(par, 2, s)
        ssq = self.ssq
        S.op("pool", lambda e: e.tensor_tensor(out=ssq[:, q0:q0 + 1], in0=ssq[:, q0:q0 + 1], in1=ssq[:, q1:q1 + 1], op=ALU.add),
             reads=[self.bssq[q0], self.bssq[q1]], writes=[self.bssq[q0]])
        self.rstd_op(q0)
        S.op("dve", lambda e: e.scalar_tensor_tensor(out=X[:, s, :], in0=tmp[:], scalar=self.rstd[:, q0:q0 + 1], in1=X[:, s, :],
                                                     op0=ALU.mult, op1=ALU.add),
             reads=[btmp[0], btmp[1], self.brstd[q0], bX], writes=[bX])


def phase_b1(nc, S, x1, g_pre, w_in, hyp_scr, cpre_scr, NSEQ, T=512, conv_tiles=None):
    NS = T // 128
    tiles_per_seq = L // T
    with contextlib.ExitStack() as st:
        sb, ps = mk_alloc(nc, st)
        NC_ = 2560
        Win = sb("Win", [128, NKC, NC_], BF16)
        bWin = S.buf("Win", True)
        w_v = w_in.rearrange("(kc p) f -> p kc f", p=128)
        S.dma("pool", [(lambda e, kc=kc: e.dma_start(out=Win[:, kc, :], in_=w_v[:, kc, 0:NC_])) for kc in range(NKC)], bWin, writes=[bWin])
        zt = sb("zt", [128, 12, PAD], BF16)
        bzt = S.buf("zt")
        S.op("pool", lambda e: e.memset(zt[:], 0.0), writes=[bzt])
        fz = []
        for sq in range(NSEQ):
            hv = hyp_scr[sq].rearrange("(j p) t -> p j t", p=128)
            cv = cpre_scr[sq].rearrange("(j p) t -> p j t", p=128)
            for (off) in (0, PAD + L):
                fz.append(lambda e, hv=hv, off=off: e.dma_start(out=hv[:, :, off:off + PAD], in_=zt[:, 0:12, :]))
                fz.append(lambda e, cv=cv, off=off: e.dma_start(out=cv[:, :, off:off + PAD], in_=zt[:, 0:4, :]))
        bpad = S.buf("padz")
        S.dma("sp", fz, bzt, reads=[bzt], writes=[bpad])
        bWin.const = False
        scale_weight_rows(nc, S, st, Win, bWin, g_pre, 32.0, NC_)
        bWin.const = True
        nf = NormFront(nc, S, st, T, n_uT=2)
        xt = [sb("xt%d" % i, [128, NS, D], F32) for i in range(3)]
        bxt = [S.buf("xt%d" % i) for i in range(3)]
        hst = [sb("hst%d" % i, [128, 12, T], BF16) for i in range(2)]
        bhst = [S.buf("hst%d" % i) for i in range(2)]
        cst_ = [sb("cstg%d" % i, [128, 4, T], BF16) for i in range(2)]
        bcst_ = [S.buf("cstg%d" % i) for i in range(2)]
        sgm = [sb("sgm%d" % i, [128, T], F32) for i in range(2)]
        bsgm = [S.buf("sgm%d" % i) for i in range(2)]
        psP = [ps("psP%d" % i, [128, 512], F32) for i in range(6)]
        bpsP = [S.buf("psP%d" % i, excl=True) for i in range(6)]
        x_v = x1.rearrange("(n s p) d -> n p s d", p=128, s=NS)
        pc = [0]
        ntiles = NSEQ * tiles_per_seq

        def ld(i):
            X, bX = xt[i % 3], bxt[i % 3]
            S.dma("act", lambda e, X=X, i=i: e.dma_start(out=X[:], in_=x_v[i]), bX, writes=[bX])

        def load(i):
            nf.front_a(xt[i % 3], bxt[i % 3], i)

        def proj(jc, pi, uT, buT):
            S.op("pe", [(lambda e, jc=jc, pi=pi, kc=kc: e.matmul(psP[pi][:, :T], lhsT=Win[:, kc, jc * 128:(jc + 1) * 128], rhs=uT[:, kc, :],
                                                                 start=(kc == 0), stop=(kc == NKC - 1))) for kc in range(NKC)],
                 reads=[bWin, buT], writes=[bpsP[pi]])

        ld(0)
        if ntiles > 1:
            ld(1)
        load(0)
        cur = nf.front_b(0)
        for it in range(ntiles):
            sq, ti = divmod(it, tiles_per_seq)
            t0 = ti * T
            uT, buT = cur
            if it + 2 < ntiles:
                ld(it + 2)
            H, bH = hst[it % 2], bhst[it % 2]
            C, bC = cst_[it % 2], bcst_[it % 2]
            do_conv = conv_tiles is None or ti < conv_tiles[sq]
            for jc in range(12):
                pi = pc[0] % 6
                pc[0] += 1
                proj(jc, pi, uT, buT)
                if jc % 2 == 0:
                    S.op("act", lambda e, pi=pi, jc=jc, H=H: e.copy(out=H[:, jc, :], in_=psP[pi][:, :T]), reads=[bpsP[pi]], writes=[bH])
                else:
                    S.op("dve", lambda e, pi=pi, jc=jc, H=H: e.tensor_copy(out=H[:, jc, :], in_=psP[pi][:, :T]), reads=[bpsP[pi]], writes=[bH])
                if jc == 1 and it + 1 < ntiles:
                    load(it + 1)
                if jc in (4, 6, 8, 10) and it + 1 < ntiles:
                    cur = nf.front_b_s(it + 1, (jc - 4) // 2)
            hv = hyp_scr[sq].rearrange("(j p) t -> p j t", p=128)
            S.dma("sp", lambda e, hv=hv, H=H, t0=t0: e.dma_start(out=hv[:, :, PAD + t0:PAD + t0 + T], in_=H[:]), bH, reads=[bH, bpad])
            if not do_conv:
                continue
            for q in range(4):
                p1 = pc[0] % 6
                p2 = (pc[0] + 1) % 6
                pc[0] += 2
                proj(12 + q, p1, uT, buT)
                proj(16 + q, p2, uT, buT)
                sb_ = q % 2
                S.op("act", lambda e, p2=p2, sb_=sb_: e.activation(out=sgm[sb_][:], in_=psP[p2][:, :T], func=AF.Sigmoid),
                     reads=[bpsP[p2]], writes=[bsgm[sb_]])
                S.op("dve", lambda e, p1=p1, sb_=sb_, q=q, C=C: e.tensor_tensor(out=C[:, q, :], in0=psP[p1][:, :T], in1=sgm[sb_][:], op=ALU.mult),
                     reads=[bpsP[p1], bsgm[sb_]], writes=[bC])
            cv = cpre_scr[sq].rearrange("(j p) t -> p j t", p=128)
            S.dma("sp", lambda e, cv=cv, C=C, t0=t0: e.dma_start(out=cv[:, :, PAD + t0:PAD + t0 + T], in_=C[:]), bC, reads=[bC, bpad])
        return S.flush()


def phase_b2(nc, S, hyp_scr, cpre_scr, hy_short_w, hy_short_b, cv_dw_w, cv_dw_b, cv_ln_g, cv_ln_b,
             vx_scr, x0_scr, cact_scr, NSEQ, T=512, own_tiles=None):
    tiles_per_seq = L // T
    KW = 31
    with contextlib.ExitStack() as st:
        sb, ps = mk_alloc(nc, st)
        ident, bid = make_ident(nc, S, st, dtype=F32, name="identf")
        bid.const = True
        psS = [ps("psS0", [128, 512], F32)] * 2
        bpsS = [S.buf("psS0", excl=True)] * 2
        psW, bpsW = psS[0], bpsS[0]
        wsh, bwsh = load_rows_T(nc, S, st, "wsh", hy_short_w, 3, NHY, ident, bid, psW, bpsW)
        wdw, bwdw = load_rows_T(nc, S, st, "wdw", cv_dw_w, KW, DH, ident, bid, psW, bpsW)
        bsh, bbsh = load_rows_T(nc, S, st, "bsh", hy_short_b.rearrange("(o c) -> o c", o=1), 1, NHY, ident, bid, psW, bpsW)
        bdw, bbdw = load_rows_T(nc, S, st, "bdw", cv_dw_b.rearrange("(o c) -> o c", o=1), 1, DH, ident, bid, psW, bpsW)
        lng, blng = load_rows_T(nc, S, st, "lng", cv_ln_g.rearrange("(o c) -> o c", o=1), 1, DH, ident, bid, psW, bpsW)
        lnb, blnb = load_rows_T(nc, S, st, "lnb", cv_ln_b.rearrange("(o c) -> o c", o=1), 1, DH, ident, bid, psW, bpsW)
        Dsh = sb("Dsh", [128, 36, 128], BF16)
        bDsh = S.buf("Dsh", True)
        Ddw = sb("Ddw", [128, 4 * KW, 128], BF16)
        bDdw = S.buf("Ddw", True)
        f = []
        for j in range(12):
            for k in range(3):
                f.append(lambda e, j=j, k=k: e.tensor_scalar(out=Dsh[:, j * 3 + k, :], in0=ident[:], scalar1=wsh[:, j, k:k + 1], scalar2=None, op0=ALU.mult))
        S.op("dve", f, reads=[bid, bwsh], writes=[bDsh])
        f = []
        for q in range(4):
            for k in range(KW):
                f.append(lambda e, q=q, k=k: e.tensor_scalar(out=Ddw[:, q * KW + k, :], in0=ident[:], scalar1=wdw[:, q, k:k + 1], scalar2=None, op0=ALU.mult))
        S.op("dve", f, reads=[bid, bwdw], writes=[bDdw])
        ones = sb("ones", [128, 128], BF16)
        bones = S.buf("ones", True)
        S.op("pool", lambda e: e.memset(ones[:], 1.0), writes=[bones])
        ceps = sb("ceps", [128, T], F32)
        cmh = sb("cmh", [128, T], F32)
        bcc = S.buf("cc", True)
        S.op("pool", [lambda e: e.memset(ceps[:], EPS), lambda e: e.memset(cmh[:], -0.5)], writes=[bcc])

        hin = [sb("hin%d" % i, [128, 12, T + 2], BF16) for i in range(3)]
        bhin = [S.buf("hin%d" % i) for i in range(3)]
        cin = [sb("cin%d" % i, [128, 4, T + 30], BF16) for i in range(3)]
        bcin = [S.buf("cin%d" % i) for i in range(3)]
        x0s = [sb("x0s%d" % i, [128, 4, T], BF16) for i in range(2)]
        bx0s = [S.buf("x0s%d" % i) for i in range(2)]
        vxs = [sb("vxs%d" % i, [128, 4, T], BF16) for i in range(2)]
        bvxs = [S.buf("vxs%d" % i) for i in range(2)]
        cas = [sb("cas%d" % i, [128, 4, T], BF16) for i in range(2)]
        bcas = [S.buf("cas%d" % i) for i in range(2)]
        x1t = [sb("x1t%d" % i, [128, T], F32) for i in range(2)]
        bx1t = [S.buf("x1t%d" % i) for i in range(2)]
        cb = [sb("cb%d" % i, [128, 4, T], BF16) for i in range(2)]
        bcb = [[S.buf("cb%d_%d" % (i, q)) for q in range(4)] for i in range(2)]
        csq = [sb("csq%d" % i, [128, 4, T], BF16) for i in range(2)]
        bcsq = [[S.buf("csq%d_%d" % (i, q)) for q in range(4)] for i in range(2)]
        mean = [sb("mean%d" % i, [128, T], F32) for i in range(2)]
        bmean = [S.buf("mean%d" % i) for i in range(2)]
        msq = sb("msq", [128, T], F32)
        bmsq = S.buf("msq")
        rstd = [sb("rstdl%d" % i, [128, T], F32) for i in range(2)]
        brstd = [S.buf("rstdl%d" % i) for i in range(2)]
        xc = [sb("xc%d" % i, [128, T], F32) for i in range(2)]
        bxc = [S.buf("xc%d" % i) for i in range(2)]
        NPC = 6
        psC = [ps("psC%d" % i, [128, 512], F32) for i in range(NPC)]
        bpsC = [S.buf("psC%d" % i, excl=True) for i in range(NPC)]
        psQ = [ps("psQ0", [128, 512], F32)] * 2
        bpsQ = [S.buf("psQ0", excl=True)] * 2
        pc = [0]
        ntiles = NSEQ * tiles_per_seq

        def is_own(it):
            sq, ti = divmod(it, tiles_per_seq)
            return own_tiles is None or ti < own_tiles[sq]

        def ld(it):
            sq, ti = divmod(it, tiles_per_seq)
            t0 = ti * T
            HI, bHI = hin[it % 3], bhin[it % 3]
            CI, bCI = cin[it % 3], bcin[it % 3]
            hv = hyp_scr[sq].rearrange("(j p) t -> p j t", p=128)
            cv = cpre_scr[sq].rearrange("(j p) t -> p j t", p=128)
            S.dma("act", lambda e, HI=HI, hv=hv, t0=t0: e.dma_start(out=HI[:], in_=hv[:, :, PAD + t0 - 1:PAD + t0 + T + 1]), bHI, writes=[bHI])
            if is_own(it):
                S.dma("act", lambda e, CI=CI, cv=cv, t0=t0: e.dma_start(out=CI[:], in_=cv[:, :, PAD + t0 - 15:PAD + t0 + T + 15]), bCI, writes=[bCI])

        def stage1(it):
            sq, ti = divmod(it, tiles_per_seq)
            t0 = ti * T
            own = is_own(it)
            HI, bHI = hin[it % 3], bhin[it % 3]
            CI, bCI = cin[it % 3], bcin[it % 3]
            X0, bX0 = x0s[it % 2], bx0s[it % 2]
            VX, bVX = vxs[it % 2], bvxs[it % 2]

            def sconv(j, pi):
                S.op("pe", [(lambda e, j=j, pi=pi, k=k, HI=HI: e.matmul(psC[pi][:, :T], lhsT=Dsh[:, j * 3 + k, :], rhs=HI[:, j, k:k + T], start=(k == 0), stop=(k == 2)))
                            for k in range(3)], reads=[bDsh, bHI], writes=[bpsC[pi]])
            for q in range(4):
                if own:
                    pi = pc[0] % NPC
                    pc[0] += 1
                    sconv(q, pi)
                    S.op("act", lambda e, pi=pi, q=q, X0=X0: e.activation(out=X0[:, q, :], in_=psC[pi][:, :T], func=AF.Identity, bias=bsh[:, q, :]),
                         reads=[bpsC[pi], bbsh], writes=[bX0])
                pi = pc[0] % NPC
                pc[0] += 1
                sconv(4 + q, pi)
                xb = q % 2
                S.op("act", lambda e, pi=pi, q=q, xb=xb: e.activation(out=x1t[xb][:], in_=psC[pi][:, :T], func=AF.Identity, bias=bsh[:, 4 + q, :]),
                     reads=[bpsC[pi], bbsh], writes=[bx1t[xb]])
                pi = pc[0] % NPC
                pc[0] += 1
                sconv(8 + q, pi)
                S.op("dve", lambda e, pi=pi, q=q, xb=xb, VX=VX: e.scalar_tensor_tensor(out=VX[:, q, :], in0=psC[pi][:, :T], scalar=bsh[:, 8 + q, :], in1=x1t[xb][:],
                                                                                        op0=ALU.add, op1=ALU.mult),
                     reads=[bpsC[pi], bbsh, bx1t[xb]], writes=[bVX])
            x0v = x0_scr[sq].rearrange("(j p) t -> p j t", p=128)
            vxv = vx_scr[sq].rearrange("(j p) t -> p j t", p=128)
            if own:
                S.dma("sp", lambda e, x0v=x0v, t0=t0, X0=X0: e.dma_start(out=x0v[:, :, t0:t0 + T], in_=X0[:]), bX0, reads=[bX0])
            S.dma("sp", lambda e, vxv=vxv, t0=t0, VX=VX: e.dma_start(out=vxv[:, :, t0:t0 + T], in_=VX[:]), bVX, reads=[bVX])
            if not own:
                return
            b2 = it % 2
            for q in range(4):
                pi = pc[0] % NPC
                pc[0] += 1
                S.op("pe", [(lambda e, q=q, pi=pi, k=k, CI=CI: e.matmul(psC[pi][:, :T], lhsT=Ddw[:, q * KW + k, :], rhs=CI[:, q, k:k + T], start=(k == 0), stop=(k == KW - 1)))
                            for k in range(KW)], reads=[bDdw, bCI], writes=[bpsC[pi]])
                S.op("act", lambda e, pi=pi, q=q, b2=b2: e.activation(out=cb[b2][:, q, :], in_=psC[pi][:, :T], func=AF.Identity, bias=bdw[:, q, :]),
                     reads=[bpsC[pi], bbdw], writes=[bcb[b2][q]])
                S.op("act", lambda e, pi=pi, q=q, b2=b2: e.activation(out=csq[b2][:, q, :], in_=psC[pi][:, :T], func=AF.Square, bias=bdw[:, q, :]),
                     reads=[bpsC[pi], bbdw], writes=[bcsq[b2][q]])

        def stage1b(it):
            if not is_own(it):
                return
            b2 = it % 2
            M, bM = mean[b2], bmean[b2]
            R, bR = rstd[b2], brstd[b2]
            S.op("pe", [(lambda e, q=q, b2=b2: e.matmul(psS[b2][:, :T], lhsT=ones[:], rhs=cb[b2][:, q, :], start=(q == 0), stop=(q == 3))) for q in range(4)],
                 reads=bcb[b2] + [bones], writes=[bpsS[b2]])
            S.op("pe", [(lambda e, q=q, b2=b2: e.matmul(psQ[b2][:, :T], lhsT=ones[:], rhs=csq[b2][:, q, :], start=(q == 0), stop=(q == 3))) for q in range(4)],
                 reads=bcsq[b2] + [bones], writes=[bpsQ[b2]])
            S.op("dve", lambda e, b2=b2, M=M: e.tensor_scalar(out=M[:], in0=psS[b2][:, :T], scalar1=1.0 / DH, scalar2=None, op0=ALU.mult), reads=[bpsS[b2]], writes=[bM])
            S.op("dve", lambda e, M=M: e.tensor_tensor(out=msq[:], in0=M[:], in1=M[:], op=ALU.mult), reads=[bM], writes=[bmsq])
            S.op("dve", lambda e, b2=b2, R=R: e.scalar_tensor_tensor(out=R[:], in0=psQ[b2][:, :T], scalar=1.0 / DH, in1=msq[:], op0=ALU.mult, op1=ALU.subtract),
                 reads=[bpsQ[b2], bmsq], writes=[bR])

        def stage2(it):
            if not is_own(it):
                return
            sq, ti = divmod(it, tiles_per_seq)
            t0 = ti * T
            b2 = it % 2
            CA, bCA = cas[it % 2], bcas[it % 2]
            M, bM = mean[b2], bmean[b2]
            R, bR = rstd[b2], brstd[b2]
            S.op("act", lambda e, R=R: e.activation(out=R[:], in_=R[:], func=AF.Sqrt, bias=ceps[:, 0:1]), reads=[bR, bcc], writes=[bR])
            S.op("dve", lambda e, R=R: e.reciprocal(out=R[:], in_=R[:]), reads=[bR], writes=[bR])
            for q in range(4):
                xb = q % 2
                S.op("dve", lambda e, q=q, xb=xb, b2=b2, M=M: e.tensor_tensor(out=xc[xb][:], in0=cb[b2][:, q, :], in1=M[:], op=ALU.subtract),
                     reads=[bcb[b2][q], bM], writes=[bxc[xb]])
                S.op("dve", lambda e, xb=xb, R=R: e.tensor_tensor(out=xc[xb][:], in0=xc[xb][:], in1=R[:], op=ALU.mult),
                     reads=[bxc[xb], bR], writes=[bxc[xb]])
                S.op("act", lambda e, q=q, xb=xb, CA=CA: e.activation(out=CA[:, q, :], in_=xc[xb][:], func=AF.Silu, bias=lnb[:, q, :], scale=lng[:, q, :]),
                     reads=[bxc[xb], blng, blnb], writes=[bCA])
            cav = cact_scr[sq].rearrange("(j p) t -> p j t", p=128)
            S.dma("sp", lambda e, cav=cav, t0=t0, CA=CA: e.dma_start(out=cav[:, :, t0:t0 + T], in_=CA[:]), bCA, reads=[bCA])

        ld(0)
        if ntiles > 1:
            ld(1)
        stage1(0)
        stage1b(0)
        for it in range(ntiles):
            if it + 2 < ntiles:
                ld(it + 2)
            if it + 1 < ntiles:
                stage1(it + 1)
            stage2(it)
            if it + 1 < ntiles:
                stage1b(it + 1)
        return S.flush()


def phase_d(nc, S, x1, x2, g_pre, w_in, hy_w_out, cv_w_out, w_out, g_post, ylong_scr, x0_scr, cact_scr, tiles, T=512):
    NS = T // 128
    with contextlib.ExitStack() as st:
        sb, ps = mk_alloc(nc, st)
        Wg = sb("Wgate", [128, NKC, 2048], BF16)
        bWg = S.buf("Wgate", True)
        w_v = w_in.rearrange("(kc p) f -> p kc f", p=128)
        S.dma("pool", [(lambda e, kc=kc: e.dma_start(out=Wg[:, kc, :], in_=w_v[:, kc, 2560:4608])) for kc in range(NKC)], bWg, writes=[bWg])
        Why = sb("Why", [128, 4, D], BF16)
        Wcv = sb("Wcv", [128, 4, D], BF16)
        Wo = sb("Wo", [128, NKC, D], BF16)
        bWs = S.buf("Wsmall", True)
        S.dma("pool", [lambda e: e.dma_start(out=Why[:], in_=hy_w_out.rearrange("(q p) f -> p q f", p=128)),
                       lambda e: e.dma_start(out=Wcv[:], in_=cv_w_out.rearrange("(q p) f -> p q f", p=128)),
                       lambda e: e.dma_start(out=Wo[:], in_=w_out.rearrange("(q p) f -> p q f", p=128))], bWs, writes=[bWs])
        gpost, bgpost = load_bcast_gain(nc, S, st, "gpost", g_post, 32.0)
        bgpost.const = True
        bWg.const = False
        scale_weight_rows(nc, S, st, Wg, bWg, g_pre, 32.0, 2048)
        bWg.const = True
        nf = NormFront(nc, S, st, T, n_uT=2)
        NB = 3
        NB2 = 2
        xt = [sb("xt%d" % i, [128, NS, D], F32) for i in range(NB)]
        bxt = [S.buf("xt%d" % i) for i in range(NB)]
        ylT = [sb("ylT%d" % i, [128, 4, T], BF16) for i in range(NB2)]
        bylT = [S.buf("ylT%d" % i) for i in range(NB2)]
        x0T = [sb("x0T%d" % i, [128, 4, T], BF16) for i in range(NB2)]
        bx0T = [S.buf("x0T%d" % i) for i in range(NB2)]
        caT = [sb("caT%d" % i, [128, 4, T], BF16) for i in range(NB2)]
        bcaT = [S.buf("caT%d" % i) for i in range(NB2)]
        hT = [sb("hT%d" % i, [128, 4, T], BF16) for i in range(NB2)]
        bhT = [S.buf("hT%d" % i) for i in range(NB2)]
        sga = [sb("sga%d" % i, [128, T], BF16) for i in range(2)]
        bsga = [S.buf("sga%d" % i) for i in range(2)]
        sgb = [sb("sgb%d" % i, [128, T], BF16) for i in range(2)]
        bsgb = [S.buf("sgb%d" % i) for i in range(2)]
        m1 = [sb("m1%d" % i, [128, T], F32) for i in range(2)]
        bm1 = [S.buf("m1%d" % i) for i in range(2)]
        m2 = [sb("m2%d" % i, [128, T], F32) for i in range(2)]
        bm2 = [S.buf("m2%d" % i) for i in range(2)]
        mT = sb("mT", [128, NKC, T], BF16)
        bmT = [S.buf("mT%d" % i) for i in range(NKC)]
        tmp = sb("tmp", [128, D], F32)
        btmp = [S.buf("tmp0"), S.buf("tmp1")]
        psGa = ps("psGa", [128, 512], F32)
        psGb = ps("psGb", [128, 512], F32)
        psA = ps("psA", [128, 512], F32)
        psB = ps("psB", [128, 512], F32)
        bpsGa, bpsGb, bpsA, bpsB = [S.buf(n, excl=True) for n in ("psGa", "psGb", "psA", "psB")]
        psM = [ps("psM%d" % i, [128, 512], F32) for i in range(2)]
        bpsM = [S.buf("psM%d" % i, excl=True) for i in range(2)]
        x_v = x1.rearrange("(n s p) d -> n p s d", p=128, s=NS)
        x2_v = x2.rearrange("(n s p) d -> n p s d", p=128, s=NS)
        tps = L // T
        n = len(tiles)

        def ld(i):
            sq, ti = tiles[i]
            row = sq * tps + ti
            t0 = ti * T
            k = i % NB2
            X, bX = xt[i % NB], bxt[i % NB]
            S.dma("act", lambda e, X=X, row=row: e.dma_start(out=X[:], in_=x_v[row]), bX, writes=[bX])
            for (dst, bdst, scr) in ((ylT[k], bylT[k], ylong_scr), (x0T[k], bx0T[k], x0_scr), (caT[k], bcaT[k], cact_scr)):
                v = scr[sq].rearrange("(j p) t -> p j t", p=128)
                S.dma("act", lambda e, dst=dst, v=v, t0=t0: e.dma_start(out=dst[:], in_=v[:, :, t0:t0 + T]), bdst, writes=[bdst])

        def load(i):
            sq, ti = tiles[i]
            row = sq * tps + ti
            t0 = ti * T
            k = i % NB2
            X, bX = xt[i % NB], bxt[i % NB]
            nf.front_a(X, bX, i)
            S.op("dve", lambda e, k=k: e.tensor_tensor(out=hT[k][:], in0=ylT[k][:], in1=x0T[k][:], op=ALU.mult),
                 reads=[bylT[k], bx0T[k]], writes=[bhT[k]])

        ld(0)
        load(0)
        cur = nf.front_b(0)
        for i in range(n):
            sq, ti = tiles[i]
            row = sq * tps + ti
            uT, buT = cur
            if i + 1 < n:
                ld(i + 1)
            X, bX = xt[i % NB], bxt[i % NB]
            H, bH = hT[i % NB2], bhT[i % NB2]
            CA, bCA = caT[i % NB2], bcaT[i % NB2]
            for dj in range(NKC):
                gb = dj % 2
                S.op("pe", [(lambda e, dj=dj, kc=kc, uT=uT: e.matmul(psGa[:, :T], lhsT=Wg[:, kc, dj * 128:(dj + 1) * 128], rhs=uT[:, kc, :],
                                                                     start=(kc == 0), stop=(kc == NKC - 1))) for kc in range(NKC)],
                     reads=[bWg, buT], writes=[bpsGa])
                S.op("pe", [(lambda e, dj=dj, kc=kc, uT=uT: e.matmul(psGb[:, :T], lhsT=Wg[:, kc, 1024 + dj * 128:1024 + (dj + 1) * 128], rhs=uT[:, kc, :],
                                                                     start=(kc == 0), stop=(kc == NKC - 1))) for kc in range(NKC)],
                     reads=[bWg, buT], writes=[bpsGb])
                S.op("pe", [(lambda e, dj=dj, q=q, H=H: e.matmul(psA[:, :T], lhsT=Why[:, q, dj * 128:(dj + 1) * 128], rhs=H[:, q, :], start=(q == 0), stop=(q == 3)))
                            for q in range(4)], reads=[bWs, bH], writes=[bpsA])
                S.op("pe", [(lambda e, dj=dj, q=q, CA=CA: e.matmul(psB[:, :T], lhsT=Wcv[:, q, dj * 128:(dj + 1) * 128], rhs=CA[:, q, :], start=(q == 0), stop=(q == 3)))
                            for q in range(4)], reads=[bWs, bCA], writes=[bpsB])
                S.op("act", lambda e, gb=gb: e.activation(out=sga[gb][:], in_=psGa[:, :T], func=AF.Sigmoid), reads=[bpsGa], writes=[bsga[gb]])
                S.op("act", lambda e, gb=gb: e.activation(out=sgb[gb][:], in_=psGb[:, :T], func=AF.Sigmoid), reads=[bpsGb], writes=[bsgb[gb]])
                S.op("dve", lambda e, gb=gb: e.tensor_tensor(out=m1[gb][:], in0=psA[:, :T], in1=sga[gb][:], op=ALU.mult), reads=[bpsA, bsga[gb]], writes=[bm1[gb]])
                S.op("dve", lambda e, gb=gb: e.tensor_tensor(out=m2[gb][:], in0=psB[:, :T], in1=sgb[gb][:], op=ALU.mult), reads=[bpsB, bsgb[gb]], writes=[bm2[gb]])
                S.op("dve", lambda e, gb=gb, dj=dj: e.tensor_tensor(out=mT[:, dj, :], in0=m1[gb][:], in1=m2[gb][:], op=ALU.add),
                     reads=[bm1[gb], bm2[gb]], writes=[bmT[dj]])
                if dj == 0 and i + 1 < n:
                    load(i + 1)
                if dj in (2, 3, 4, 5) and i + 1 < n:
                    cur = nf.front_b_s(i + 1, dj - 2)
            for s in range(NS):
                for h in range(2):
                    S.op("pe", [(lambda e, s=s, h=h, dj=dj: e.matmul(psM[h][:], lhsT=mT[:, dj, s * 128:(s + 1) * 128],
                                                                     rhs=Wo[:, dj, h * 512:(h + 1) * 512], start=(dj == 0), stop=(dj == NKC - 1)))
                                for dj in range(NKC)], reads=bmT + [bWs], writes=[bpsM[h]])
                    nf.post_half(psM[h], bpsM[h], gpost, bgpost, tmp, btmp, s, i, h)
                nf.post_fin(tmp, btmp, X, bX, s, i)
            S.dma("sp", lambda e, X=X, row=row: e.dma_start(out=x2_v[row], in_=X[:]), bX, reads=[bX])
        return S.flush()

import contextlib
import numpy as np


def ffn_phase(nc, S, x_src, x_dst, g_pre, wg, wu, wd, g_post, NT, T=256):
    assert NT % T == 0 and T % 128 == 0
    NS = T // 128
    ntiles = NT // T
    with contextlib.ExitStack() as st:
        sb, ps = mk_alloc(nc, st)
        Wg = sb("Wg", [128, NKC, DFF], BF16)
        Wu = sb("Wu", [128, NKC, DFF], BF16)
        Wd = sb("Wd", [128, NJ, D], BF16)
        bWg, bWu, bWd = S.buf("Wg", True), S.buf("Wu", True), S.buf("Wd", True)
        wg_v = wg.rearrange("(kc p) f -> p kc f", p=128)
        wu_v = wu.rearrange("(kc p) f -> p kc f", p=128)
        wd_v = wd.rearrange("(j p) f -> p j f", p=128)
        S.dma("pool", [(lambda e, kc=kc: e.dma_start(out=Wg[:, kc, :], in_=wg_v[:, kc, :])) for kc in range(NKC)], bWg, writes=[bWg])
        S.dma("pool", [(lambda e, kc=kc: e.dma_start(out=Wu[:, kc, :], in_=wu_v[:, kc, :])) for kc in range(NKC)], bWu, writes=[bWu])
        S.dma("pool", [(lambda e, j=j: e.dma_start(out=Wd[:, 2 * j:2 * j + 2, :], in_=wd_v[:, 2 * j:2 * j + 2, :])) for j in range(NJ // 2)], bWd, writes=[bWd])
        gpost, bgpost = load_bcast_gain(nc, S, st, "gpost", g_post, 16.0)
        bgpost.const = True
        bWg.const = False
        bWu.const = False
        scale_weight_rows(nc, S, st, Wg, bWg, g_pre, 32.0, DFF)
        scale_weight_rows(nc, S, st, Wu, bWu, g_pre, 32.0, DFF)
        bWg.const = True
        bWu.const = True
        nf = NormFront(nc, S, st, T, n_uT=2)
        NB = 3
        xt = [sb("xt%d" % i, [128, NS, D], F32) for i in range(NB)]
        bxt = [S.buf("xt%d" % i) for i in range(NB)]
        aT = sb("aT", [128, NJ, T], BF16)
        baT = [S.buf("aT%d" % j) for j in range(NJ)]
        sg = [sb("sg%d" % i, [128, T], F32) for i in range(2)]
        bsg = [S.buf("sg%d" % i) for i in range(2)]
        tmp = sb("tmp", [128, D], F32)
        btmp = [S.buf("tmp0"), S.buf("tmp1")]
        psG = [ps("psG%d" % i, [128, 512], F32) for i in range(2)]
        psU = [ps("psU%d" % i, [128, 512], F32) for i in range(2)]
        bpsG = [S.buf("psG%d" % i, excl=True) for i in range(2)]
        bpsU = [S.buf("psU%d" % i, excl=True) for i in range(2)]
        psD = [ps("psD%d" % i, [128, 512], F32) for i in range(2)]
        bpsD = [S.buf("psD%d" % i, excl=True) for i in range(2)]
        x_src_v = x_src.rearrange("(n s p) d -> n p s d", p=128, s=NS)
        x_dst_v = x_dst.rearrange("(n s p) d -> n p s d", p=128, s=NS)
        gc = [0]

        def ld(i):
            X, bX = xt[i % NB], bxt[i % NB]
            S.dma("act", lambda e, X=X, i=i: e.dma_start(out=X[:], in_=x_src_v[i]), bX, writes=[bX])

        def load(i):
            nf.front_a(xt[i % NB], bxt[i % NB], i)

        def gu(i, j, uT, buT):
            gb = gc[0] % 2
            gc[0] += 1
            S.op("pe", [(lambda e, gb=gb, j=j, kc=kc: e.matmul(psG[gb][:, :T], lhsT=Wg[:, kc, j * 128:(j + 1) * 128], rhs=uT[:, kc, :],
                                                               start=(kc == 0), stop=(kc == NKC - 1))) for kc in range(NKC)],
                 reads=[bWg, buT], writes=[bpsG[gb]])
            S.op("pe", [(lambda e, gb=gb, j=j, kc=kc: e.matmul(psU[gb][:, :T], lhsT=Wu[:, kc, j * 128:(j + 1) * 128], rhs=uT[:, kc, :],
                                                               start=(kc == 0), stop=(kc == NKC - 1))) for kc in range(NKC)],
                 reads=[bWu, buT], writes=[bpsU[gb]])
            S.op("act", lambda e, gb=gb: e.activation(out=sg[gb][:], in_=psG[gb][:, :T], func=AF.Silu), reads=[bpsG[gb]], writes=[bsg[gb]])
            S.op("dve", lambda e, gb=gb, j=j: e.tensor_tensor(out=aT[:, j, :], in0=psU[gb][:, :T], in1=sg[gb][:], op=ALU.mult),
                 reads=[bpsU[gb], bsg[gb]], writes=[baT[j]])

        def down(i, s):
            X, bX = xt[i % NB], bxt[i % NB]
            for h in range(2):
                S.op("pe", [(lambda e, s=s, h=h, j=j: e.matmul(psD[h][:], lhsT=aT[:, j, s * 128:(s + 1) * 128],
                                                               rhs=Wd[:, j, h * 512:(h + 1) * 512], start=(j == 0), stop=(j == NJ - 1)))
                            for j in range(NJ)], reads=baT + [bWd], writes=[bpsD[h]])
                nf.post_half(psD[h], bpsD[h], gpost, bgpost, tmp, btmp, s, i, h)
            nf.post_fin(tmp, btmp, X, bX, s, i)

        ld(0)
        if ntiles > 1:
            ld(1)
        load(0)
        cur = nf.front_b(0)
        for i in range(ntiles):
            uT, buT = cur
            if i + 2 < ntiles:
                ld(i + 2)
            for j in range(NJ):
                gu(i, j, uT, buT)
                if j == 4 and i + 1 < ntiles:
                    load(i + 1)
                if j in (10, 15) and i + 1 < ntiles:
                    cur = nf.front_b_s(i + 1, (j - 10) // 5)
            for s in range(NS):
                down(i, s)
            X, bX = xt[i % NB], bxt[i % NB]
            S.dma("sp", lambda e, X=X, i=i: e.dma_start(out=x_dst_v[i], in_=X[:]), bX, reads=[bX])
        return S.flush()

import contextlib
import math
import numpy as np
import ml_dtypes

NFFT = 2 * L
CB = 32
NBLK = 512 // CB
FH = 64


def host_consts():
    bf = ml_dtypes.bfloat16
    a = np.arange(128)
    c = {}
    ang = -2 * np.pi * np.outer(a, a) / 128.0
    Er, Ei = np.cos(ang), np.sin(ang)
    c["F1d"] = np.concatenate([np.concatenate([Er[:64], Ei[:64]], 1), np.concatenate([-Ei[:64], Er[:64]], 1)], 0).astype(bf)
    c["F1k"] = np.concatenate([Er, Ei], 1).astype(bf)
    k = a[None, :, None] + 128 * a[None, None, :]
    angg = -2 * np.pi * (a[:, None, None] * k) / NFFT
    c["Gr"] = np.cos(angg).astype(bf)
    c["Gi"] = np.sin(angg).astype(bf)
    angh = 2 * np.pi * np.outer(a, a) / 128.0
    Hr, Hi = np.cos(angh), np.sin(angh)
    c["H1"] = np.concatenate([Hr, Hi], 1).astype(bf)
    c["H2"] = np.concatenate([-Hi, Hr], 1).astype(bf)
    th = np.arange(64)
    t = a[None, :, None] + 128 * th[None, None, :]
    angp = 2 * np.pi * (a[:, None, None] * t) / NFFT
    Gpr, Gpi = np.cos(angp), np.sin(angp)
    c["T1"] = np.concatenate([Gpr, Gpi], 2).astype(bf)
    c["T2"] = np.concatenate([-Gpi, Gpr], 2).astype(bf)
    n = np.arange(NFFT)
    j = np.where(n <= L, n, NFFT - n).astype(np.float64)
    j[L] = 0
    tj = j / (L - 1)
    wj = 2 * np.pi * j / L
    bands = np.linspace(1e-4, 15.0, 16)
    angf = wj[:, None] * bands[None, :]
    z = np.concatenate([tj[:, None], np.cos(angf), -np.sin(angf)], 1)
    c["zT"] = np.ascontiguousarray(z.T).astype(np.float32)
    tneg = -tj.copy()
    tneg[L] = -1e4
    c["tneg"] = np.ascontiguousarray(tneg.reshape(128, 128)).astype(np.float32)
    max_decay = math.log(1e-2) / 0.3
    min_decay = math.log(1e-2) / 1.5
    c["absdelta"] = np.abs(np.linspace(min_decay, max_decay, 512)).astype(np.float32)
    return c


CONST_SPECS = {"F1d": ([128, 256], BF16), "F1k": ([128, 256], BF16), "Gr": ([128, 128, 128], BF16), "Gi": ([128, 128, 128], BF16),
               "H1": ([128, 256], BF16), "H2": ([128, 256], BF16), "T1": ([128, 128, 128], BF16), "T2": ([128, 128, 128], BF16),
               "zT": ([33, NFFT], F32), "tneg": ([128, 128], F32), "absdelta": ([512], F32)}


def load_const(nc, S, st, name, dram, shape, dt, queue="sp"):
    t = st.enter_context(nc.sbuf_tensor(uname(name), shape, dt))
    b = S.buf(name, True)
    S.dma(queue, lambda e: e.dma_start(out=t[:], in_=dram), b, writes=[b])
    return t, b


class Fwd:
    def __init__(self, nc, S, st, cd, f1name):
        sb, ps = mk_alloc(nc, st)
        self.S = S
        self.F1, self.bF1 = load_const(nc, S, st, f1name, cd[f1name], [128, 256], BF16)
        self.Gr, self.bGr = load_const(nc, S, st, "Gr", cd["Gr"], [128, 128, 128], BF16)
        self.Gi, self.bGi = load_const(nc, S, st, "Gi", cd["Gi"], [128, 128, 128], BF16)
        self.A2 = [sb("A%d" % i, [128, CB, 3, 128], BF16) for i in range(2)]
        self.bA2 = [S.buf("A%d" % i) for i in range(2)]
        self.Xs2 = [sb("Xs%d" % i, [128, 2, 128, CB], BF16) for i in range(2)]
        self.bXs2 = [S.buf("Xs%d" % i) for i in range(2)]
        self.xcnt = 0
        self.ps1 = [ps("psF1_%d" % i, [128, 2, 256], F32) for i in range(2)]
        self.bps1 = [S.buf("psF1_%d" % i, excl=True) for i in range(2)]
        self.ps2 = [ps("psF2_%d" % i, [128, 8, 2, CB], F32) for i in range(2)]
        self.bps2 = [S.buf("psF2_%d" % i, excl=True) for i in range(2)]
        self.cnt = 0

    def run(self, Zin, bZin):
        S = self.S
        A, bA = self.A2[self.xcnt % 2], self.bA2[self.xcnt % 2]
        Xs, bXs = self.Xs2[self.xcnt % 2], self.bXs2[self.xcnt % 2]
        self.xcnt += 1
        self.Xs, self.bXs = Xs, bXs
        for c2 in range(CB // 2):
            pi = self.cnt % 2
            self.cnt += 1
            P, bP = self.ps1[pi], self.bps1[pi]
            S.op("pe", [(lambda e, P=P, c2=c2, i=i: e.matmul(P[:, i, :], lhsT=Zin[:, 2 * c2 + i, :], rhs=self.F1[:], start=True, stop=True)) for i in range(2)],
                 reads=[bZin, self.bF1], writes=[bP])
            S.op("act", lambda e, P=P, c2=c2, A=A: e.copy(out=A[:, 2 * c2:2 * c2 + 2, 0:2, :], in_=P.rearrange("p c (r k) -> p c r k", r=2)),
                 reads=[bP], writes=[bA])
            S.op("act", lambda e, P=P, c2=c2, A=A: e.activation(out=A[:, 2 * c2:2 * c2 + 2, 2, :], in_=P[:, :, 128:256], func=AF.Copy, scale=-1.0),
                 reads=[bP], writes=[bA])
        for g in range(16):
            pi = self.cnt % 2
            self.cnt += 1
            P, bP = self.ps2[pi], self.bps2[pi]
            fns = []
            for kk in range(8):
                ka = g * 8 + kk
                fns.append(lambda e, P=P, kk=kk, ka=ka, A=A: e.matmul(P[:, kk, 0, :], lhsT=self.Gr[:, ka, :], rhs=A[:, :, 0, ka], start=True, stop=False))
                fns.append(lambda e, P=P, kk=kk, ka=ka, A=A: e.matmul(P[:, kk, 0, :], lhsT=self.Gi[:, ka, :], rhs=A[:, :, 2, ka], start=False, stop=True))
                fns.append(lambda e, P=P, kk=kk, ka=ka, A=A: e.matmul(P[:, kk, 1, :], lhsT=self.Gi[:, ka, :], rhs=A[:, :, 0, ka], start=True, stop=False))
                fns.append(lambda e, P=P, kk=kk, ka=ka, A=A: e.matmul(P[:, kk, 1, :], lhsT=self.Gr[:, ka, :], rhs=A[:, :, 1, ka], start=False, stop=True))
            S.op("pe", fns, reads=[bA, self.bGr, self.bGi], writes=[bP])
            eng = "act"
            if eng == "act":
                S.op("act", lambda e, P=P, g=g, Xs=Xs: e.copy(out=Xs[:, :, g * 8:(g + 1) * 8, :], in_=P.rearrange("p k r c -> p r k c")), reads=[bP], writes=[bXs])
            else:
                S.op("dve", lambda e, P=P, g=g, Xs=Xs: e.tensor_copy(out=Xs[:, :, g * 8:(g + 1) * 8, :], in_=P.rearrange("p k r c -> p r k c")), reads=[bP], writes=[bXs])


def phase_c0(nc, S, cd, w1, b1, f1, w2, b2, f2, w3, hy_bias, kt_scr, kspec):
    TWO_PI = 2 * math.pi
    with contextlib.ExitStack() as st:
        sb, ps = mk_alloc(nc, st)
        w1s, bw1 = load_const(nc, S, st, "w1s", w1, [33, FH], F32)
        w2s, bw2 = load_const(nc, S, st, "w2s", w2, [FH, FH], F32)
        w3s = sb("w3s", [FH, 1024], BF16)
        bw3 = S.buf("w3s", True)
        S.dma("pool", lambda e: e.dma_start(out=w3s[:], in_=w3), bw3, writes=[bw3])
        w3sum = sb("w3sum", [FH, 512], BF16)
        bw3sum = S.buf("w3sum", True)
        S.op("dve", lambda e: e.tensor_tensor(out=w3sum[:], in0=w3s[:, 0:512], in1=w3s[:, 512:1024], op=ALU.add), reads=[bw3], writes=[bw3sum])
        col = sb("fcol", [FH, 8], F32)
        bcol = S.buf("fcol", True)
        S.dma("sp", [lambda e: e.dma_start(out=col[:, 0:1], in_=b1.rearrange("(p o) -> p o", o=1)),
                     lambda e: e.dma_start(out=col[:, 1:2], in_=f1.rearrange("(p o) -> p o", o=1)),
                     lambda e: e.dma_start(out=col[:, 2:3], in_=b2.rearrange("(p o) -> p o", o=1)),
                     lambda e: e.dma_start(out=col[:, 3:4], in_=f2.rearrange("(p o) -> p o", o=1))], bcol, writes=[bcol])
        S.op("dve", lambda e: e.tensor_tensor(out=col[:, 4:5], in0=col[:, 0:1], in1=col[:, 1:2], op=ALU.mult), reads=[bcol], writes=[bcol])
        S.op("dve", lambda e: e.tensor_tensor(out=col[:, 5:6], in0=col[:, 2:3], in1=col[:, 3:4], op=ALU.mult), reads=[bcol], writes=[bcol])
        S.op("pool", lambda e: e.memset(col[:, 6:7], -math.pi), reads=[bcol], writes=[bcol])
        h2T = sb("h2T", [FH, NFFT], BF16)
        bh2T = S.buf("h2T")
        st2 = contextlib.ExitStack()
        sb2, ps2 = mk_alloc(nc, st2)
        rt = [sb2("rrt%d" % i, [FH, 512], F32) for i in range(4)]
        brt = [S.buf("rrt%d" % i) for i in range(4)]
        rrc = [0]
        MAGIC = 12582912.0

        def rr(a, ba):
            i = rrc[0] % 4
            rrc[0] += 1
            t, bt = rt[i], brt[i]
            S.op("dve", lambda e: e.tensor_scalar(out=t[:], in0=a[:], scalar1=1.0 / TWO_PI, scalar2=MAGIC, op0=ALU.mult, op1=ALU.add), reads=[ba], writes=[bt])
            S.op("dve", lambda e: e.tensor_scalar(out=t[:], in0=t[:], scalar1=MAGIC, scalar2=TWO_PI, op0=ALU.subtract, op1=ALU.mult), reads=[bt], writes=[bt])
            S.op("dve", lambda e: e.tensor_tensor(out=a[:], in0=a[:], in1=t[:], op=ALU.subtract), reads=[ba, bt], writes=[ba])
            S.op("dve", lambda e: e.tensor_scalar(out=a[:], in0=a[:], scalar1=3.1415925, scalar2=-3.1415925, op0=ALU.min, op1=ALU.max), reads=[ba], writes=[ba])
        NMB = 3
        zc = [sb2("zc%d" % i, [33, 512], F32) for i in range(NMB)]
        bzc = [S.buf("zc%d" % i) for i in range(NMB)]
        a1 = [sb2("a1_%d" % i, [FH, 512], F32) for i in range(NMB)]
        ba1 = [S.buf("a1_%d" % i) for i in range(NMB)]
        h1 = [sb2("h1_%d" % i, [FH, 512], F32) for i in range(NMB)]
        bh1 = [S.buf("h1_%d" % i) for i in range(NMB)]
        a2 = [sb2("a2_%d" % i, [FH, 512], F32) for i in range(NMB)]
        ba2 = [S.buf("a2_%d" % i) for i in range(NMB)]
        psm = [ps2("psm%d" % i, [FH, 512], F32) for i in range(2 * NMB)]
        bpsm = [S.buf("psm%d" % i, excl=True) for i in range(2 * NMB)]
        zT = cd["zT"]
        for ci in range(NFFT // 512):
            b = ci % NMB
            n0 = ci * 512
            S.dma("sp", lambda e, b=b, n0=n0: e.dma_start(out=zc[b][:], in_=zT[:, n0:n0 + 512]), bzc[b], writes=[bzc[b]])
            p1, p2 = psm[2 * b], psm[2 * b + 1]
            bp1, bp2 = bpsm[2 * b], bpsm[2 * b + 1]
            S.op("pe", lambda e, b=b, p1=p1: e.matmul(p1[:], lhsT=w1s[:], rhs=zc[b][:], start=True, stop=True), reads=[bw1, bzc[b]], writes=[bp1])
            S.op("act", lambda e, b=b, p1=p1: e.activation(out=a1[b][:], in_=p1[:], func=AF.Identity, bias=col[:, 4:5], scale=col[:, 1:2]),
                 reads=[bp1, bcol], writes=[ba1[b]])
            rr(a1[b], ba1[b])
            S.op("act", lambda e, b=b: e.activation(out=h1[b][:], in_=a1[b][:], func=AF.Sin), reads=[ba1[b]], writes=[bh1[b]])
            S.op("pe", lambda e, b=b, p2=p2: e.matmul(p2[:], lhsT=w2s[:], rhs=h1[b][:], start=True, stop=True), reads=[bw2, bh1[b]], writes=[bp2])
            S.op("act", lambda e, b=b, p2=p2: e.activation(out=a2[b][:], in_=p2[:], func=AF.Identity, bias=col[:, 5:6], scale=col[:, 3:4]),
                 reads=[bp2, bcol], writes=[ba2[b]])
            rr(a2[b], ba2[b])
            S.op("act", lambda e, b=b, n0=n0: e.activation(out=h2T[:, n0:n0 + 512], in_=a2[b][:], func=AF.Sin),
                 reads=[ba2[b]], writes=[bh2T])
        S.flush(keep=True)
        st2.close()
        kt = sb("kt", [128, 512, 128], BF16)
        bkt = S.buf("kt")
        tneg, btneg = load_const(nc, S, st, "tneg", cd["tneg"], [128, 128], F32)
        adl = sb("adl", [128, 512], F32)
        badl = S.buf("adl", True)
        S.dma("sp", lambda e: e.dma_start(out=adl[:], in_=cd["absdelta"].partition_broadcast(128)), badl, writes=[badl])
        kacc = sb("kacc", [128, 512], F32)
        bkacc = S.buf("kacc")
        S.op("pool", lambda e: e.memset(kacc[:], 0.0), writes=[bkacc])
        Ex = [sb("Ex%d" % i, [128, 512], F32) for i in range(2)]
        bEx = [S.buf("Ex%d" % i) for i in range(2)]
        kf = [sb("kf%d" % i, [128, 512], F32) for i in range(2)]
        bkf = [S.buf("kf%d" % i) for i in range(2)]
        sq = [sb("sq%d" % i, [128, 512], F32) for i in range(2)]
        bsq = [S.buf("sq%d" % i) for i in range(2)]
        psK = [ps("psK%d" % i, [128, 512], F32) for i in range(2)]
        bpsK = [S.buf("psK%d" % i, excl=True) for i in range(2)]
        h3 = h2T.rearrange("p (a b) -> p a b", b=128)
        for tl in range(128):
            b = tl % 2
            P, bP = psK[b], bpsK[b]
            fns = [lambda e, P=P, tl=tl: e.matmul(P[0:64, :], lhsT=h3[:, 0:64, tl], rhs=w3s[:, 0:512], start=True, stop=True),
                   lambda e, P=P, tl=tl: e.matmul(P[64:128, :], lhsT=h3[:, 64:128, tl], rhs=w3s[:, 512:1024], start=True, stop=True, tile_position=(0, 64))]
            if tl == 0:
                fns.append(lambda e, P=P: e.matmul(P[0:1, :], lhsT=h2T[:, 0:1], rhs=w3sum[:], start=True, stop=True))
            S.op("pe", fns, reads=[bh2T, bw3, bw3sum], writes=[bP])
            S.op("act", lambda e, b=b, tl=tl: e.activation(out=Ex[b][:], in_=adl[:], func=AF.Exp, scale=tneg[:, tl:tl + 1]),
                 reads=[badl, btneg], writes=[bEx[b]])
            S.op("dve", lambda e, b=b, P=P: e.tensor_tensor(out=kf[b][:], in0=P[:], in1=Ex[b][:], op=ALU.mult), reads=[bP, bEx[b]], writes=[bkf[b]])
            if tl % 2 == 0:
                S.op("act", lambda e, b=b, tl=tl: e.copy(out=kt[:, :, tl], in_=kf[b][:]), reads=[bkf[b]], writes=[bkt])
            else:
                S.op("pool", lambda e, b=b, tl=tl: e.tensor_copy(out=kt[:, :, tl], in_=kf[b][:]), reads=[bkf[b]], writes=[bkt])
            S.op("dve", lambda e, b=b: e.tensor_tensor(out=sq[b][:], in0=kf[b][:], in1=kf[b][:], op=ALU.mult), reads=[bkf[b]], writes=[bsq[b]])
            S.op("dve", lambda e, b=b: e.tensor_tensor(out=kacc[:], in0=kacc[:], in1=sq[b][:], op=ALU.add), reads=[bkacc, bsq[b]], writes=[bkacc])
        S.dma("sp", lambda e: e.dma_start(out=kt_scr, in_=kt[:]), bkt, reads=[bkt])
        onesf = sb("onesf", [128, 128], F32)
        bones = S.buf("onesf", True)
        S.op("pool", lambda e: e.memset(onesf[:], 1.0), writes=[bones])
        S.op("pe", lambda e: e.matmul(psK[0][:], lhsT=onesf[:], rhs=kacc[:], start=True, stop=True), reads=[bones, bkacc], writes=[bpsK[0]])
        rn = sb("rn", [128, 512], F32)
        brn = S.buf("rn")
        cE = sb("cE", [128, 512], F32)
        cM = sb("cM", [128, 512], F32)
        bcE = S.buf("cE", True)
        S.op("pool", [lambda e: e.memset(cE[:], EPS), lambda e: e.memset(cM[:], -0.5)], writes=[bcE])
        S.op("dve", lambda e: e.tensor_copy(out=rn[:], in_=psK[0][:]), reads=[bpsK[0]], writes=[brn])
        S.op("act", lambda e: e.activation(out=rn[:], in_=rn[:], func=AF.Sqrt, bias=cE[:, 0:1]), reads=[brn, bcE], writes=[brn])
        S.op("dve", lambda e: e.reciprocal(out=rn[:], in_=rn[:]), reads=[brn], writes=[brn])
        S.op("act", lambda e: e.mul(rn[:], rn[:], 1.0 / NFFT), reads=[brn], writes=[brn])
        S.dma("sp", lambda e: e.dma_start(out=cd["rn_scr"], in_=rn[0:1, :]), brn, reads=[brn])
        S.flush()
    with contextlib.ExitStack() as st:
        sb, ps = mk_alloc(nc, st)
        fw = Fwd(nc, S, st, cd, "F1k")
        rn = sb("rn2", [128, 512], F32)
        brn = S.buf("rn2", True)
        S.dma("sp", lambda e: e.dma_start(out=rn[:], in_=cd["rn_scr"].rearrange("o c -> (o c)").partition_broadcast(128)), brn, writes=[brn])
        hb = sb("hb", [128, 512], F32)
        bhb = S.buf("hb", True)
        S.dma("sp", lambda e: e.dma_start(out=hb[:], in_=hy_bias.partition_broadcast(128)), bhb, writes=[bhb])
        S.op("act", lambda e: e.mul(hb[:], hb[:], 1.0 / NFFT), reads=[bhb], writes=[bhb])
        Zin = [sb("Zin%d" % i, [128, CB, 128], BF16) for i in range(2)]
        bZin = [S.buf("Zin%d" % i) for i in range(2)]
        Kt = [sb("Kt%d" % i, [128, 2, 128, CB], BF16) for i in range(2)]
        bKt = [S.buf("Kt%d" % i) for i in range(2)]
        ktv = kt_scr.rearrange("p (c b) -> p c b", b=128)
        def ldz(blk):
            b = blk % 2
            c0 = blk * CB
            S.dma("act", lambda e, b=b, c0=c0: e.dma_start(out=Zin[b][:], in_=ktv[:, c0:c0 + CB, :]), bZin[b], writes=[bZin[b]])
        ldz(0)
        for blk in range(NBLK):
            b = blk % 2
            c0 = blk * CB
            if blk + 1 < NBLK:
                ldz(blk + 1)
            fw.run(Zin[b], bZin[b])
            rnb = rn[:, c0:c0 + CB].unsqueeze(1).unsqueeze(1).broadcast_to([128, 2, 128, CB])
            hbb = hb[:, c0:c0 + CB].unsqueeze(1).broadcast_to([128, 128, CB])
            XS = fw.Xs
            S.op("dve", lambda e, b=b, rnb=rnb, XS=XS: e.tensor_tensor(out=Kt[b][:], in0=XS[:], in1=rnb, op=ALU.mult), reads=[fw.bXs, brn], writes=[bKt[b]])
            S.op("dve", lambda e, b=b, hbb=hbb: e.tensor_tensor(out=Kt[b][:, 0, :, :], in0=Kt[b][:, 0, :, :], in1=hbb, op=ALU.add), reads=[bKt[b], bhb], writes=[bKt[b]])
            S.dma("sp", lambda e, b=b, blk=blk: e.dma_start(out=kspec[blk], in_=Kt[b].rearrange("p r a c -> p (r a c)")), bKt[b], reads=[bKt[b]])
        S.flush()


def phase_c1(nc, S, cd, vx_scr, kspec, yspec):
    with contextlib.ExitStack() as st:
        sb, ps = mk_alloc(nc, st)
        fw = Fwd(nc, S, st, cd, "F1d")
        Zin = [sb("Zin%d" % i, [128, CB, 128], BF16) for i in range(2)]
        bZin = [S.buf("Zin%d" % i) for i in range(2)]
        Kt = [sb("Kt0", [128, 2, 128, CB], BF16)] * 2
        bKt = [S.buf("Kt0")] * 2
        Y = [sb("Y0", [128, 2, 128, CB], BF16)] * 2
        bY = [S.buf("Y0")] * 2
        tt = [sb("tt0", [128, 128, CB], BF16)]
        btt = [S.buf("tt0")]
        def ldz(blk):
            b = blk % 2
            c0 = blk * CB
            S.dma("act", [(lambda e, b=b, c0=c0, r=r: e.dma_start(out=Zin[b][r * 64:(r + 1) * 64, :, :],
                                                                   in_=vx_scr[r][c0:c0 + CB].rearrange("c (a t) -> a c t", t=128))) for r in range(2)],
                  bZin[b], writes=[bZin[b]])
        ldz(0)
        for blk in range(NBLK):
            b = blk % 2
            c0 = blk * CB
            if blk + 1 < NBLK:
                ldz(blk + 1)
            S.dma("sp", lambda e, b=b, blk=blk: e.dma_start(out=Kt[b].rearrange("p r a c -> p (r a c)"), in_=kspec[blk]), bKt[b], writes=[bKt[b]])
            fw.run(Zin[b], bZin[b])
            Xre, Xim = fw.Xs[:, 0, :, :], fw.Xs[:, 1, :, :]
            Kre, Kim = Kt[b][:, 0, :, :], Kt[b][:, 1, :, :]
            Yre = Y[b][:, 0, :, :]
            Yim = Y[b][:, 1, :, :]
            S.op("dve", lambda e, Xre=Xre, Kre=Kre, Yre=Yre: e.tensor_tensor(out=Yre, in0=Xre, in1=Kre, op=ALU.mult), reads=[fw.bXs, bKt[b]], writes=[bY[b]])
            S.op("dve", lambda e, Xim=Xim, Kim=Kim: e.tensor_tensor(out=tt[0][:], in0=Xim, in1=Kim, op=ALU.mult), reads=[fw.bXs, bKt[b]], writes=[btt[0]])
            S.op("dve", lambda e, Yre=Yre: e.tensor_tensor(out=Yre, in0=Yre, in1=tt[0][:], op=ALU.subtract), reads=[btt[0], bY[b]], writes=[bY[b]])
            S.op("dve", lambda e, Xre=Xre, Kim=Kim, Yim=Yim: e.tensor_tensor(out=Yim, in0=Xre, in1=Kim, op=ALU.mult), reads=[fw.bXs, bKt[b]], writes=[bY[b]])
            S.op("dve", lambda e, Xim=Xim, Kre=Kre: e.tensor_tensor(out=tt[0][:], in0=Xim, in1=Kre, op=ALU.mult), reads=[fw.bXs, bKt[b]], writes=[btt[0]])
            S.op("dve", lambda e, Yim=Yim: e.tensor_tensor(out=Yim, in0=Yim, in1=tt[0][:], op=ALU.add), reads=[btt[0], bY[b]], writes=[bY[b]])
            S.dma("sp", lambda e, b=b, blk=blk: e.dma_start(out=yspec[blk], in_=Y[b].rearrange("p r k c -> p (r k c)")), bY[b], reads=[bY[b]])
        S.flush()


def phase_c2(nc, S, cd, yspec, ylong_scr):
    with contextlib.ExitStack() as st:
        sb, ps = mk_alloc(nc, st)
        H1, bH1 = load_const(nc, S, st, "H1", cd["H1"], [128, 256], BF16)
        H2, bH2 = load_const(nc, S, st, "H2", cd["H2"], [128, 256], BF16)
        T1, bT1 = load_const(nc, S, st, "T1", cd["T1"], [128, 128, 128], BF16)
        T2, bT2 = load_const(nc, S, st, "T2", cd["T2"], [128, 128, 128], BF16)
        Y = [sb("Yi%d" % i, [128, 2, 128, CB], BF16) for i in range(2)]
        bY = [S.buf("Yi%d" % i) for i in range(2)]
        B = sb("B", [128, CB, 2, 128], BF16)
        bB = S.buf("B")
        Yo = [sb("Yo%d" % i, [128, CB, 128], BF16) for i in range(2)]
        bYo = [S.buf("Yo%d" % i) for i in range(2)]
        ps1 = [ps("psI1_%d" % i, [128, 2, 256], F32) for i in range(2)]
        bps1 = [S.buf("psI1_%d" % i, excl=True) for i in range(2)]
        ps2 = [ps("psI2_%d" % i, [128, 16, CB], F32) for i in range(2)]
        bps2 = [S.buf("psI2_%d" % i, excl=True) for i in range(2)]
        cnt = 0
        def ldy(blk):
            b = blk % 2
            S.dma("act", lambda e, b=b, blk=blk: e.dma_start(out=Y[b].rearrange("p r k c -> p (r k c)"), in_=yspec[blk]), bY[b], writes=[bY[b]])
        ldy(0)
        for blk in range(NBLK):
            b = blk % 2
            c0 = blk * CB
            if blk + 1 < NBLK:
                ldy(blk + 1)
            for c2 in range(CB // 2):
                pi = cnt % 2
                cnt += 1
                P, bP = ps1[pi], bps1[pi]
                fns = []
                for i in range(2):
                    c = 2 * c2 + i
                    fns.append(lambda e, P=P, i=i, c=c, b=b: e.matmul(P[:, i, :], lhsT=Y[b][:, 0, :, c], rhs=H1[:], start=True, stop=False))
                    fns.append(lambda e, P=P, i=i, c=c, b=b: e.matmul(P[:, i, :], lhsT=Y[b][:, 1, :, c], rhs=H2[:], start=False, stop=True))
                S.op("pe", fns, reads=[bY[b], bH1, bH2], writes=[bP])
                if c2 % 2 == 0:
                    S.op("act", lambda e, P=P, c2=c2: e.copy(out=B[:, 2 * c2:2 * c2 + 2, :, :], in_=P.rearrange("p c (r k) -> p c r k", r=2)), reads=[bP], writes=[bB])
                else:
                    S.op("dve", lambda e, P=P, c2=c2: e.tensor_copy(out=B[:, 2 * c2:2 * c2 + 2, :, :], in_=P.rearrange("p c (r k) -> p c r k", r=2)), reads=[bP], writes=[bB])
            for g in range(8):
                pi = cnt % 2
                cnt += 1
                P, bP = ps2[pi], bps2[pi]
                fns = []
                for tt_ in range(16):
                    tl = g * 16 + tt_
                    fns.append(lambda e, P=P, tt_=tt_, tl=tl: e.matmul(P[:, tt_, :], lhsT=T1[:, tl, :], rhs=B[:, :, 0, tl], start=True, stop=False))
                    fns.append(lambda e, P=P, tt_=tt_, tl=tl: e.matmul(P[:, tt_, :], lhsT=T2[:, tl, :], rhs=B[:, :, 1, tl], start=False, stop=True))
                S.op("pe", fns, reads=[bB, bT1, bT2], writes=[bP])
                if g % 2 == 0:
                    S.op("act", lambda e, P=P, g=g, b=b: e.copy(out=Yo[b][:, :, g * 16:(g + 1) * 16], in_=P.rearrange("p t c -> p c t")), reads=[bP], writes=[bYo[b]])
                else:
                    S.op("dve", lambda e, P=P, g=g, b=b: e.tensor_copy(out=Yo[b][:, :, g * 16:(g + 1) * 16], in_=P.rearrange("p t c -> p c t")), reads=[bP], writes=[bYo[b]])
            S.dma("sp", [(lambda e, b=b, c0=c0, r=r: e.dma_start(out=ylong_scr[r][c0:c0 + CB].rearrange("c (a t) -> a c t", t=128),
                                                                   in_=Yo[b][r * 64:(r + 1) * 64, :, :])) for r in range(2)], bYo[b], reads=[bYo[b]])
        S.flush()

from concourse.bass_utils import run_bass_kernel_spmd

NCORES = 8
NSEQ_CORE = 2
NT_CORE = NSEQ_CORE * L
W_SPECS = [
    ("ffn1_norm_pre", [D]), ("ffn1_w_gate", [D, DFF]), ("ffn1_w_up", [D, DFF]), ("ffn1_w_down", [DFF, D]), ("ffn1_norm_post", [D]),
    ("mix_norm_pre", [D]), ("w_in", [D, 4608]), ("hy_short_w", [3, 1536]), ("hy_short_b", [1536]),
    ("hy_filt_w1", [33, 64]), ("hy_filt_b1", [64]), ("hy_filt_freq1", [64]), ("hy_filt_w2", [64, 64]), ("hy_filt_b2", [64]),
    ("hy_filt_freq2", [64]), ("hy_filt_w3", [64, 1024]), ("hy_bias", [512]), ("hy_w_out", [512, D]),
    ("cv_dw_w", [31, 512]), ("cv_dw_b", [512]), ("cv_ln_g", [512]), ("cv_ln_b", [512]), ("cv_w_out", [512, D]),
    ("w_out", [D, D]), ("mix_norm_post", [D]),
    ("ffn2_norm_pre", [D]), ("ffn2_w_gate", [D, DFF]), ("ffn2_w_up", [D, DFF]), ("ffn2_w_down", [DFF, D]), ("ffn2_norm_post", [D]),
]


NT_OUT = L + L // 2
TPS = L // 512


def build_program():
    nc = bass.Bass("TRN2", target_bir_lowering=False)

    def inp(name, shape, dt=F32):
        return nc.dram_tensor(name, list(shape), dt, kind="ExternalInput").ap()

    def scr(name, shape, dt=BF16):
        return nc.dram_tensor(name, list(shape), dt, kind="Internal").ap()
    x = inp("x", [NT_CORE, D])
    y = nc.dram_tensor("y", [NT_OUT, D], F32, kind="ExternalOutput").ap()
    w = {n: inp(n, s) for n, s in W_SPECS}
    cd = {k: inp("c_" + k, *CONST_SPECS[k]) for k in CONST_SPECS}
    cd["rn_scr"] = scr("rn_scr", [1, 512], F32)
    x1 = scr("x1_scr", [NT_CORE, D], F32)
    x2 = scr("x2_scr", [NT_CORE, D], F32)
    hyp = scr("hyp_scr", [NSEQ_CORE, NHY, LP])
    cpre = scr("cpre_scr", [NSEQ_CORE, DH, LP])
    vx = scr("vx_scr", [NSEQ_CORE, DH, L])
    x0 = scr("x0_scr", [NSEQ_CORE, DH, L])
    cact = scr("cact_scr", [NSEQ_CORE, DH, L])
    ylong = scr("ylong_scr", [NSEQ_CORE, DH, L])
    kt_scr = scr("kt_scr", [128, 512 * 128])
    kspec = scr("kspec", [NBLK, 128, 128 * 2 * CB])
    yspec = scr("yspec", [NBLK, 128, 2 * CB * 128])
    own = [TPS, TPS // 2]
    with contextlib.ExitStack() as st:
        S = Sched(nc, st)
        phase_c0(nc, S, cd, w["hy_filt_w1"], w["hy_filt_b1"], w["hy_filt_freq1"], w["hy_filt_w2"], w["hy_filt_b2"], w["hy_filt_freq2"],
                 w["hy_filt_w3"], w["hy_bias"], kt_scr, kspec)
        ffn_phase(nc, S, x, x1, w["ffn1_norm_pre"], w["ffn1_w_gate"], w["ffn1_w_up"], w["ffn1_w_down"], w["ffn1_norm_post"], NT_CORE, T=256)
        phase_b1(nc, S, x1, w["mix_norm_pre"], w["w_in"], hyp, cpre, NSEQ_CORE, conv_tiles=[own[0], own[1] + 1])
        phase_b2(nc, S, hyp, cpre, w["hy_short_w"], w["hy_short_b"], w["cv_dw_w"], w["cv_dw_b"], w["cv_ln_g"], w["cv_ln_b"],
                 vx, x0, cact, NSEQ_CORE, own_tiles=own)
        phase_c1(nc, S, cd, vx, kspec, yspec)
        phase_c2(nc, S, cd, yspec, ylong)
        phase_d(nc, S, x1, x2, w["mix_norm_pre"], w["w_in"], w["hy_w_out"], w["cv_w_out"], w["w_out"], w["mix_norm_post"],
                ylong, x0, cact, [(sq, ti) for sq in range(NSEQ_CORE) for ti in range(own[sq])])
        ffn_phase(nc, S, x2[0:NT_OUT], y, w["ffn2_norm_pre"], w["ffn2_w_gate"], w["ffn2_w_up"], w["ffn2_w_down"], w["ffn2_norm_post"], NT_OUT, T=256)
    return nc


def kernel(**inputs):
    xp = np.asarray(inputs["x_prompt"], dtype=np.float32)
    xs = np.asarray(inputs["x_sample"], dtype=np.float32)
    nown, nsh = xp.shape[0], xs.shape[0]
    consts = host_consts()
    base = {n: np.ascontiguousarray(np.asarray(inputs[n], dtype=np.float32)[0]) for n, _ in W_SPECS}
    rev = dict(base)
    rev["hy_short_w"] = np.ascontiguousarray(base["hy_short_w"][::-1])
    rev["cv_dw_w"] = np.ascontiguousarray(base["cv_dw_w"][::-1])
    w3 = base["hy_filt_w3"]
    rev["hy_filt_w3"] = np.ascontiguousarray(np.concatenate([w3[:, 512:], w3[:, :512]], axis=1))
    for k, v in consts.items():
        base["c_" + k] = v
        rev["c_" + k] = v
    in_maps = []
    for c in range(NCORES):
        odd = c % 2 == 1
        a, b = xp[c], xs[c // 2]
        if odd:
            a, b = a[::-1], b[::-1]
        m = dict(rev if odd else base)
        m["x"] = np.ascontiguousarray(np.concatenate([a, b], axis=0))
        in_maps.append(m)
    nc = build_program()
    res = run_bass_kernel_spmd(nc, in_maps, core_ids=list(range(NCORES)))
    y_prompt = np.empty((nown, L, D), np.float32)
    y_sample = np.empty((nsh, L, D), np.float32)
    H = L // 2
    for c in range(NCORES):
        yc = res.results[c]["y"]
        if c % 2 == 0:
            y_prompt[c] = yc[:L]
            y_sample[c // 2][:H] = yc[L:]
        else:
            y_prompt[c] = yc[:L][::-1]
            y_sample[c // 2][H:] = yc[L:][::-1]
    return (y_prompt, y_sample)
```

```python
import contextlib
import numpy as np
import concourse.bass as bass
import concourse.mybir as mybir

F32 = mybir.dt.float32
BF16 = mybir.dt.bfloat16
AF = mybir.ActivationFunctionType
ALU = mybir.AluOpType

ENGS = ["pe", "act", "dve", "pool", "sp"]


class Buf:
    __slots__ = ("name", "lw", "rd", "dsem", "dcnt", "const", "excl")

    def __init__(self, name, const=False, excl=False):
        self.excl = excl
        self.name = name
        self.lw = None
        self.rd = []
        self.dsem = None
        self.dcnt = 0
        self.const = const


class Op:
    __slots__ = ("eng", "fns", "waits", "sig", "ordinal", "dma", "ev")

    def __init__(self, eng, fns, dma):
        self.eng = eng
        self.fns = fns
        self.waits = []
        self.sig = False
        self.ordinal = None
        self.dma = dma
        self.ev = None


class Sched:
    def __init__(self, nc, stack, n_dma_sems=48):
        self.nc = nc
        self.esem = {e: stack.enter_context(nc.semaphore("es_" + e)) for e in ENGS}
        self.bar = stack.enter_context(nc.semaphore("barrier"))
        self.dsems = [stack.enter_context(nc.semaphore("ds%d" % i)) for i in range(n_dma_sems)]
        self.dsem_cnt = [0] * n_dma_sems
        self.dsem_free = list(range(n_dma_sems))
        self.ecount = {e: 0 for e in ENGS}
        self.barcount = 0
        self.ops = {e: [] for e in ENGS}
        self.bufs = []

    def buf(self, name, const=False, excl=False):
        b = Buf(name, const, excl)
        self.bufs.append(b)
        return b

    def _alloc_dsem(self, b):
        if b.dsem is None:
            b.dsem = self.dsem_free.pop(0)
        return b.dsem

    def _deps(self, reads, writes):
        deps = []
        for b in reads:
            if b.lw is not None:
                deps.append(b.lw)
        for b in writes:
            if b.lw is not None:
                deps.append(b.lw)
            deps.extend(b.rd)
        return deps

    def _finish(self, op, ev, reads, writes):
        for b in writes:
            b.lw = ev
            b.rd = []
        for b in reads:
            if b in writes or b.const:
                continue
            b.rd.append(ev)

    def op(self, eng, fns, reads=(), writes=()):
        if callable(fns):
            fns = [fns]
        ex = [b for b in reads if b.excl]
        if ex:
            reads = [b for b in reads if not b.excl]
            writes = list(writes) + ex
        o = Op(eng, fns, False)
        for d in self._deps(reads, writes):
            self._add_wait(o, d)
        ev = ("c", o)
        o.ev = ev
        self.ops[eng].append(o)
        self._finish(o, ev, reads, writes)
        return o

    def dma(self, queue, fns, sbuf, reads=(), writes=()):
        if callable(fns):
            fns = [fns]
        o = Op(queue, fns, True)
        for d in self._deps(reads, writes):
            self._add_wait(o, d)
        si = self._alloc_dsem(sbuf)
        self.dsem_cnt[si] += 16 * len(fns)
        ev = ("d", si, self.dsem_cnt[si])
        o.ev = ev
        self.ops[queue].append(o)
        self._finish(o, ev, reads, writes)
        return o

    def _add_wait(self, o, d):
        if d[0] == "c":
            d[1].sig = True
        o.waits.append(d)

    def flush(self, final=False, keep=False):
        nc = self.nc
        last_ops = {}
        for e in ENGS:
            for o in reversed(self.ops[e]):
                if not o.dma:
                    o.sig = True
                    last_ops[e] = o
                    break
        for e in ENGS:
            for o in self.ops[e]:
                if (not o.dma) and o.sig:
                    self.ecount[e] += 1
                    o.ordinal = self.ecount[e]
        self.barcount += 1
        barval = self.barcount
        dma_final = [(i, c) for i, c in enumerate(self.dsem_cnt) if c > 0]
        ecount = dict(self.ecount)
        known0 = getattr(self, "_known", {e: {x: 0 for x in ENGS} for e in ENGS})
        kd0 = getattr(self, "_kd", {e: [0] * len(self.dsems) for e in ENGS})

        def emit(engobj, e):
            known = known0[e]
            kd = kd0[e]
            for o in self.ops[e]:
                for d in o.waits:
                    if d[0] == "c":
                        po = d[1]
                        if po.ordinal > known[po.eng]:
                            engobj.wait_ge(self.esem[po.eng], po.ordinal)
                            known[po.eng] = po.ordinal
                    else:
                        _, si, cnt = d
                        if cnt > kd[si]:
                            engobj.wait_ge(self.dsems[si], cnt)
                            kd[si] = cnt
                n = len(o.fns)
                for i, fn in enumerate(o.fns):
                    ins = fn(engobj)
                    if o.dma:
                        ins.then_inc(self.dsems[o.ev[1]], 16)
                    elif o.sig and i == n - 1:
                        ins.then_inc(self.esem[e], 1)
            if e == "sp":
                for x in ENGS:
                    if ecount[x] > known[x]:
                        engobj.wait_ge(self.esem[x], ecount[x])
                        known[x] = ecount[x]
                for si, c in dma_final:
                    if c > kd[si]:
                        engobj.wait_ge(self.dsems[si], c)
                        kd[si] = c
                engobj.sem_inc(self.bar, 1)
            engobj.wait_ge(self.bar, barval)
            for x in ENGS:
                known[x] = ecount[x]
            for si, c in dma_final:
                kd[si] = c

        with nc.Block() as block:
            block.tensor(lambda eo: emit(eo, "pe"))
            block.scalar(lambda eo: emit(eo, "act"))
            block.vector(lambda eo: emit(eo, "dve"))
            block.gpsimd(lambda eo: emit(eo, "pool"))
            block.sync(lambda eo: emit(eo, "sp"))
        self._known = known0
        self._kd = kd0
        n_ops = {e: len(self.ops[e]) for e in ENGS}
        self.ops = {e: [] for e in ENGS}
        for b in self.bufs:
            b.lw = None
            b.rd = []
            if b.dsem is not None:
                self.dsem_free.append(b.dsem)
                b.dsem = None
        if not keep:
            self.bufs = []
        return n_ops

import contextlib
import numpy as np

D = 1024
DFF = 2816
NKC = D // 128
NJ = DFF // 128
EPS = 1e-6


_uid = [0]


def uname(name):
    _uid[0] += 1
    return "%s_%d" % (name, _uid[0])


def load_bcast_gain(nc, S, st, name, g_dram, scale):
    n = g_dram.shape[-1]
    t = st.enter_context(nc.sbuf_tensor(uname(name), [128, n], F32))
    b = S.buf(name)
    S.dma("sp", lambda e: e.dma_start(out=t[:], in_=g_dram.partition_broadcast(128)), b, writes=[b])
    if scale != 1.0:
        S.op("act", lambda e: e.mul(t[:], t[:], float(scale)), reads=[b], writes=[b])
    return t, b


def make_ident(nc, S, st, dtype=BF16, name="ident"):
    idf = st.enter_context(nc.sbuf_tensor(uname(name + "_f"), [128, 128], F32))
    idt = st.enter_context(nc.sbuf_tensor(uname(name), [128, 128], dtype))
    b = S.buf(name)

    def f1(e):
        return e.memset(idf[:], 1.0)

    def f2(e):
        return e.affine_select(out=idf[:], in_=idf[:], pattern=[[-1, 128]], compare_op=ALU.is_equal,
                               fill=0.0, base=0, channel_multiplier=1)

    def f3(e):
        return e.tensor_copy(out=idt[:], in_=idf[:])
    S.op("pool", f1, writes=[b])
    S.op("pool", f2, reads=[b], writes=[b])
    S.op("pool", f3, reads=[b], writes=[b])
    return idt, b

import contextlib
import numpy as np

L = 8192
PAD = 32
LP = L + 2 * PAD
DH = 512
NHY = 1536


def mk_alloc(nc, st):
    sb = lambda name, shape, dt: st.enter_context(nc.sbuf_tensor(uname(name), shape, dt))
    ps = lambda name, shape, dt: st.enter_context(nc.psum_tensor(uname(name), shape, dt))
    return sb, ps


def load_rows_T(nc, S, st, name, mat_dram, R, C, identf, bidentf, psum_t, bpsum_t):
    n = C // 128
    rows = st.enter_context(nc.sbuf_tensor(uname(name + "_rows"), [R, C], F32))
    brows = S.buf(name + "_rows")
    S.dma("sp", lambda e: e.dma_start(out=rows[:], in_=mat_dram), brows, writes=[brows])
    t = st.enter_context(nc.sbuf_tensor(uname(name), [128, n, R], F32))
    b = S.buf(name, True)
    for j in range(n):
        S.op("pe", lambda e, j=j: e.transpose(out=psum_t[:, 0:R], in_=rows[:, j * 128:(j + 1) * 128], identity=identf[0:R, 0:R]),
             reads=[brows, bidentf], writes=[bpsum_t])
        S.op("dve", lambda e, j=j: e.tensor_copy(out=t[:, j, :], in_=psum_t[:, 0:R]), reads=[bpsum_t], writes=[b])
    return t, b


def scale_weight_rows(nc, S, st, W, bW, g_dram, factor, width):
    gc = st.enter_context(nc.sbuf_tensor(uname("gcol"), [128, NKC], F32))
    bgc = S.buf("gcol")
    S.dma("sp", lambda e: e.dma_start(out=gc[:], in_=g_dram.rearrange("(kc p) -> p kc", p=128), allow_slow_non_contiguous=True), bgc, writes=[bgc])
    S.op("act", lambda e: e.mul(gc[:], gc[:], float(factor)), reads=[bgc], writes=[bgc])
    for kc in range(NKC):
        if kc % 2 == 0:
            S.op("dve", lambda e, kc=kc: e.tensor_scalar(out=W[:, kc, 0:width], in0=W[:, kc, 0:width], scalar1=gc[:, kc:kc + 1], scalar2=None, op0=ALU.mult),
                 reads=[bgc, bW], writes=[bW])
        else:
            S.op("act", lambda e, kc=kc: e.activation(out=W[:, kc, 0:width], in_=W[:, kc, 0:width], func=AF.Copy, scale=gc[:, kc:kc + 1]),
                 reads=[bgc, bW], writes=[bW])


class NormFront:
    def __init__(self, nc, S, st, T, n_uT=2):
        sb, ps = mk_alloc(nc, st)
        self.nc, self.S, self.T, self.NS = nc, S, T, T // 128
        NS = self.NS
        self.ident, self.bid = make_ident(nc, S, st, dtype=F32, name="identn")
        self.bid.const = True
        self.cst = sb("cst", [128, 2], F32)
        self.bcst = S.buf("cst", True)
        S.op("pool", [lambda e: e.memset(self.cst[:, 0:1], float(D * EPS)), lambda e: e.memset(self.cst[:, 1:2], -0.5)], writes=[self.bcst])
        self.junk = sb("junk", [128, D], BF16)
        self.bjunk = S.buf("junk")
        ncol = 6 * NS
        self.ssq = sb("ssq", [128, ncol], F32)
        self.rstd = sb("rstd", [128, ncol], F32)
        self.bssq = [S.buf("ssq%d" % i) for i in range(ncol)]
        self.brstd = [S.buf("rstd%d" % i) for i in range(ncol)]
        self.xb = [sb("xb%d" % i, [128, D], BF16) for i in range(NS)]
        self.bxb = [S.buf("xb%d" % i) for i in range(NS)]
        self.Rd = [sb("Rd%d" % i, [128, 128], BF16) for i in range(NS)]
        self.bRd = [S.buf("Rd%d" % i) for i in range(NS)]
        self.n_uT = n_uT
        self.uT = [sb("uT%d" % i, [128, NKC, T], BF16) for i in range(n_uT)]
        self.buT = [S.buf("uT%d" % i) for i in range(n_uT)]
        self.psT = [ps("psT%d" % i, [128, NKC // 2, 128], F32) for i in range(2)]
        self.bpsT = [S.buf("psT%d" % i, excl=True) for i in range(2)]

    def col(self, par, post, s):
        return (par % 2) * 3 * self.NS + post * self.NS + s

    def rstd_op(self, col):
        S = self.S
        ssq, rstd, cst = self.ssq, self.rstd, self.cst
        S.op("pool", lambda e: e.tensor_tensor(out=rstd[:, col:col + 1], in0=ssq[:, col:col + 1], in1=cst[:, 0:1], op=ALU.add),
             reads=[self.bssq[col], self.bcst], writes=[self.brstd[col]])
        S.op("pool", lambda e: e.tensor_tensor(out=rstd[:, col:col + 1], in0=rstd[:, col:col + 1], in1=cst[:, 1:2], op=ALU.pow),
             reads=[self.brstd[col], self.bcst], writes=[self.brstd[col]])

    def front_a(self, X, bX, par):
        S = self.S
        for s in range(self.NS):
            c = self.col(par, 0, s)
            S.op("act", lambda e, s=s, c=c: e.activation(out=self.junk[:], in_=X[:, s, :], func=AF.Square, accum_out=self.ssq[:, c:c + 1]),
                 reads=[bX], writes=[self.bjunk, self.bssq[c]])
            self.rstd_op(c)
            S.op("dve", lambda e, s=s, c=c: e.tensor_scalar(out=self.Rd[s][:], in0=self.ident[:], scalar1=self.rstd[:, c:c + 1], scalar2=None, op0=ALU.mult),
                 reads=[self.bid, self.brstd[c]], writes=[self.bRd[s]])
            S.op("dve", lambda e, s=s: e.tensor_copy(out=self.xb[s][:], in_=X[:, s, :]), reads=[bX], writes=[self.bxb[s]])

    def front_b_s(self, par, s):
        S = self.S
        uT, buT = self.uT[par % self.n_uT], self.buT[par % self.n_uT]
        H = NKC // 2
        for hf in range(2):
            P, bP = self.psT[hf], self.bpsT[hf]
            S.op("pe", [(lambda e, s=s, kc=kc, P=P, hf=hf: e.matmul(P[:, kc - hf * H, :], lhsT=self.xb[s][:, kc * 128:(kc + 1) * 128], rhs=self.Rd[s][:], start=True, stop=True))
                        for kc in range(hf * H, (hf + 1) * H)], reads=[self.bxb[s], self.bRd[s]], writes=[bP])
            S.op("act", lambda e, s=s, uT=uT, P=P, hf=hf: e.copy(out=uT[:, hf * H:(hf + 1) * H, s * 128:(s + 1) * 128], in_=P[:]),
                 reads=[bP], writes=[buT])
        return uT, buT

    def front_b(self, par):
        for s in range(self.NS):
            r = self.front_b_s(par, s)
        return r

    def post_half(self, psDh, bpsDh, gpost, bgpost, tmp, btmp, s, par, h):
        S = self.S
        q = self.col(par, 1 + h, s)
        S.op("act", lambda e: e.activation(out=self.junk[:, 0:512], in_=psDh[:], func=AF.Square, accum_out=self.ssq[:, q:q + 1]),
             reads=[bpsDh], writes=[self.bjunk, self.bssq[q]])
        S.op("dve", lambda e: e.tensor_tensor(out=tmp[:, h * 512:(h + 1) * 512], in0=psDh[:], in1=gpost[:, h * 512:(h + 1) * 512], op=ALU.mult),
             reads=[bpsDh, bgpost], writes=[btmp[h]])

    def post_fin(self, tmp, btmp, X, bX, s, par):
        S = self.S
        q0, q1 = self.col(par, 1, s), self.col(par, 2, s)
        ssq = self.ssq
        S.op("pool", lambda e: e.tensor_tensor(out=ssq[:, q0:q0 + 1], in0=ssq[:, q0:q0 + 1], in1=ssq[:, q1:q1 + 1], op=ALU.add),
             reads=[self.bssq[q0], self.bssq[q1]], writes=[self.bssq[q0]])
        self.rstd_op(q0)
        S.op("dve", lambda e: e.scalar_tensor_tensor(out=X[:, s, :], in0=tmp[:], scalar=self.rstd[:, q0:q0 + 1], in1=X[:, s, :],
                                                     op0=ALU.mult, op1=ALU.add),
             reads=[btmp[0], btmp[1], self.brstd[q0], bX], writes=[bX])


def phase_b1(nc, S, x1, g_pre, w_in, hyp_scr, cpre_scr, NSEQ, T=512, conv_tiles=None):
    NS = T // 128
    tiles_per_seq = L // T
    with contextlib.ExitStack() as st:
        sb, ps = mk_alloc(nc, st)
        NC_ = 2560
        Win = sb("Win", [128, NKC, NC_], BF16)
        bWin = S.buf("Win", True)
        w_v = w_in.rearrange("(kc p) f -> p kc f", p=128)
        S.dma("pool", [(lambda e, kc=kc: e.dma_start(out=Win[:, kc, :], in_=w_v[:, kc, 0:NC_])) for kc in range(NKC)], bWin, writes=[bWin])
        zt = sb("zt", [128, 12, PAD], BF16)
        bzt = S.buf("zt")
        S.op("pool", lambda e: e.memset(zt[:], 0.0), writes=[bzt])
        fz = []
        for sq in range(NSEQ):
            hv = hyp_scr[sq].rearrange("(j p) t -> p j t", p=128)
            cv = cpre_scr[sq].rearrange("(j p) t -> p j t", p=128)
            for (off) in (0, PAD + L):
                fz.append(lambda e, hv=hv, off=off: e.dma_start(out=hv[:, :, off:off + PAD], in_=zt[:, 0:12, :]))
                fz.append(lambda e, cv=cv, off=off: e.dma_start(out=cv[:, :, off:off + PAD], in_=zt[:, 0:4, :]))
        bpad = S.buf("padz")
        S.dma("sp", fz, bzt, reads=[bzt], writes=[bpad])
        bWin.const = False
        scale_weight_rows(nc, S, st, Win, bWin, g_pre, 32.0, NC_)
        bWin.const = True
        nf = NormFront(nc, S, st, T, n_uT=2)
        xt = [sb("xt%d" % i, [128, NS, D], F32) for i in range(3)]
        bxt = [S.buf("xt%d" % i) for i in range(3)]
        hst = [sb("hst%d" % i, [128, 12, T], BF16) for i in range(2)]
        bhst = [S.buf("hst%d" % i) for i in range(2)]
        cst_ = [sb("cstg%d" % i, [128, 4, T], BF16) for i in range(2)]
        bcst_ = [S.buf("cstg%d" % i) for i in range(2)]
        sgm = [sb("sgm%d" % i, [128, T], F32) for i in range(2)]
        bsgm = [S.buf("sgm%d" % i) for i in range(2)]
        psP = [ps("psP%d" % i, [128, 512], F32) for i in range(6)]
        bpsP = [S.buf("psP%d" % i, excl=True) for i in range(6)]
        x_v = x1.rearrange("(n s p) d -> n p s d", p=128, s=NS)
        pc = [0]
        ntiles = NSEQ * tiles_per_seq

        def ld(i):
            X, bX = xt[i % 3], bxt[i % 3]
            S.dma("act", lambda e, X=X, i=i: e.dma_start(out=X[:], in_=x_v[i]), bX, writes=[bX])

        def load(i):
            nf.front_a(xt[i % 3], bxt[i % 3], i)

        def proj(jc, pi, uT, buT):
            S.op("pe", [(lambda e, jc=jc, pi=pi, kc=kc: e.matmul(psP[pi][:, :T], lhsT=Win[:, kc, jc * 128:(jc + 1) * 128], rhs=uT[:, kc, :],
                                                                 start=(kc == 0), stop=(kc == NKC - 1))) for kc in range(NKC)],
                 reads=[bWin, buT], writes=[bpsP[pi]])

        ld(0)
        if ntiles > 1:
            ld(1)
        load(0)
        cur = nf.front_b(0)
        for it in range(ntiles):
            sq, ti = divmod(it, tiles_per_seq)
            t0 = ti * T
            uT, buT = cur
            if it + 2 < ntiles:
                ld(it + 2)
            H, bH = hst[it % 2], bhst[it % 2]
            C, bC = cst_[it % 2], bcst_[it % 2]
            do_conv = conv_tiles is None or ti < conv_tiles[sq]
            for jc in range(12):
                pi = pc[0] % 6
                pc[0] += 1
                proj(jc, pi, uT, buT)
                if jc % 2 == 0:
                    S.op("act", lambda e, pi=pi, jc=jc, H=H: e.copy(out=H[:, jc, :], in_=psP[pi][:, :T]), reads=[bpsP[pi]], writes=[bH])
                else:
                    S.op("dve", lambda e, pi=pi, jc=jc, H=H: e.tensor_copy(out=H[:, jc, :], in_=psP[pi][:, :T]), reads=[bpsP[pi]], writes=[bH])
                if jc == 1 and it + 1 < ntiles:
                    load(it + 1)
                if jc in (4, 6, 8, 10) and it + 1 < ntiles:
                    cur = nf.front_b_s(it + 1, (jc - 4) // 2)
            hv = hyp_scr[sq].rearrange("(j p) t -> p j t", p=128)
            S.dma("sp", lambda e, hv=hv, H=H, t0=t0: e.dma_start(out=hv[:, :, PAD + t0:PAD + t0 + T], in_=H[:]), bH, reads=[bH, bpad])
            if not do_conv:
                continue
            for q in range(4):
                p1 = pc[0] % 6
                p2 = (pc[0] + 1) % 6
                pc[0] += 2
                proj(12 + q, p1, uT, buT)
                proj(16 + q, p2, uT, buT)
                sb_ = q % 2
                S.op("act", lambda e, p2=p2, sb_=sb_: e.activation(out=sgm[sb_][:], in_=psP[p2][:, :T], func=AF.Sigmoid),
                     reads=[bpsP[p2]], writes=[bsgm[sb_]])
                S.op("dve", lambda e, p1=p1, sb_=sb_, q=q, C=C: e.tensor_tensor(out=C[:, q, :], in0=psP[p1][:, :T], in1=sgm[sb_][:], op=ALU.mult),
                     reads=[bpsP[p1], bsgm[sb_]], writes=[bC])
            cv = cpre_scr[sq].rearrange("(j p) t -> p j t", p=128)
            S.dma("sp", lambda e, cv=cv, C=C, t0=t0: e.dma_start(out=cv[:, :, PAD + t0:PAD + t0 + T], in_=C[:]), bC, reads=[bC, bpad])
        return S.flush()


def phase_b2(nc, S, hyp_scr, cpre_scr, hy_short_w, hy_short_b, cv_dw_w, cv_dw_b, cv_ln_g, cv_ln_b,
             vx_scr, x0_scr, cact_scr, NSEQ, T=512, own_tiles=None):
    tiles_per_seq = L // T
    KW = 31
    with contextlib.ExitStack() as st:
        sb, ps = mk_alloc(nc, st)
        ident, bid = make_ident(nc, S, st, dtype=F32, name="identf")
        bid.const = True
        psS = [ps("psS0", [128, 512], F32)] * 2
        bpsS = [S.buf("psS0", excl=True)] * 2
        psW, bpsW = psS[0], bpsS[0]
        wsh, bwsh = load_rows_T(nc, S, st, "wsh", hy_short_w, 3, NHY, ident, bid, psW, bpsW)
        wdw, bwdw = load_rows_T(nc, S, st, "wdw", cv_dw_w, KW, DH, ident, bid, psW, bpsW)
        bsh, bbsh = load_rows_T(nc, S, st, "bsh", hy_short_b.rearrange("(o c) -> o c", o=1), 1, NHY, ident, bid, psW, bpsW)
        bdw, bbdw = load_rows_T(nc, S, st, "bdw", cv_dw_b.rearrange("(o c) -> o c", o=1), 1, DH, ident, bid, psW, bpsW)
        lng, blng = load_rows_T(nc, S, st, "lng", cv_ln_g.rearrange("(o c) -> o c", o=1), 1, DH, ident, bid, psW, bpsW)
        lnb, blnb = load_rows_T(nc, S, st, "lnb", cv_ln_b.rearrange("(o c) -> o c", o=1), 1, DH, ident, bid, psW, bpsW)
        Dsh = sb("Dsh", [128, 36, 128], BF16)
        bDsh = S.buf("Dsh", True)
        Ddw = sb("Ddw", [128, 4 * KW, 128], BF16)
        bDdw = S.buf("Ddw", True)
        f = []
        for j in range(12):
            for k in range(3):
                f.append(lambda e, j=j, k=k: e.tensor_scalar(out=Dsh[:, j * 3 + k, :], in0=ident[:], scalar1=wsh[:, j, k:k + 1], scalar2=None, op0=ALU.mult))
        S.op("dve", f, reads=[bid, bwsh], writes=[bDsh])
        f = []
        for q in range(4):
            for k in range(KW):
                f.append(lambda e, q=q, k=k: e.tensor_scalar(out=Ddw[:, q * KW + k, :], in0=ident[:], scalar1=wdw[:, q, k:k + 1], scalar2=None, op0=ALU.mult))
        S.op("dve", f, reads=[bid, bwdw], writes=[bDdw])
        ones = sb("ones", [128, 128], BF16)
        bones = S.buf("ones", True)
        S.op("pool", lambda e: e.memset(ones[:], 1.0), writes=[bones])
        ceps = sb("ceps", [128, T], F32)
        cmh = sb("cmh", [128, T], F32)
        bcc = S.buf("cc", True)
        S.op("pool", [lambda e: e.memset(ceps[:], EPS), lambda e: e.memset(cmh[:], -0.5)], writes=[bcc])

        hin = [sb("hin%d" % i, [128, 12, T + 2], BF16) for i in range(3)]
        bhin = [S.buf("hin%d" % i) for i in range(3)]
        cin = [sb("cin%d" % i, [128, 4, T + 30], BF16) for i in range(3)]
        bcin = [S.buf("cin%d" % i) for i in range(3)]
        x0s = [sb("x0s%d" % i, [128, 4, T], BF16) for i in range(2)]
        bx0s = [S.buf("x0s%d" % i) for i in range(2)]
        vxs = [sb("vxs%d" % i, [128, 4, T], BF16) for i in range(2)]
        bvxs = [S.buf("vxs%d" % i) for i in range(2)]
        cas = [sb("cas%d" % i, [128, 4, T], BF16) for i in range(2)]
        bcas = [S.buf("cas%d" % i) for i in range(2)]
        x1t = [sb("x1t%d" % i, [128, T], F32) for i in range(2)]
        bx1t = [S.buf("x1t%d" % i) for i in range(2)]
        cb = [sb("cb%d" % i, [128, 4, T], BF16) for i in range(2)]
        bcb = [[S.buf("cb%d_%d" % (i, q)) for q in range(4)] for i in range(2)]
        csq = [sb("csq%d" % i, [128, 4, T], BF16) for i in range(2)]
        bcsq = [[S.buf("csq%d_%d" % (i, q)) for q in range(4)] for i in range(2)]
        mean = [sb("mean%d" % i, [128, T], F32) for i in range(2)]
        bmean = [S.buf("mean%d" % i) for i in range(2)]
        msq = sb("msq", [128, T], F32)
        bmsq = S.buf("msq")
        rstd = [sb("rstdl%d" % i, [128, T], F32) for i in range(2)]
        brstd = [S.buf("rstdl%d" % i) for i in range(2)]
        xc = [sb("xc%d" % i, [128, T], F32) for i in range(2)]
        bxc = [S.buf("xc%d" % i) for i in range(2)]
        NPC = 6
        psC = [ps("psC%d" % i, [128, 512], F32) for i in range(NPC)]
        bpsC = [S.buf("psC%d" % i, excl=True) for i in range(NPC)]
        psQ = [ps("psQ0", [128, 512], F32)] * 2
        bpsQ = [S.buf("psQ0", excl=True)] * 2
        pc = [0]
        ntiles = NSEQ * tiles_per_seq

        def is_own(it):
            sq, ti = divmod(it, tiles_per_seq)
            return own_tiles is None or ti < own_tiles[sq]

        def ld(it):
            sq, ti = divmod(it, tiles_per_seq)
            t0 = ti * T
            HI, bHI = hin[it % 3], bhin[it % 3]
            CI, bCI = cin[it % 3], bcin[it % 3]
            hv = hyp_scr[sq].rearrange("(j p) t -> p j t", p=128)
            cv = cpre_scr[sq].rearrange("(j p) t -> p j t", p=128)
            S.dma("act", lambda e, HI=HI, hv=hv, t0=t0: e.dma_start(out=HI[:], in_=hv[:, :, PAD + t0 - 1:PAD + t0 + T + 1]), bHI, writes=[bHI])
            if is_own(it):
                S.dma("act", lambda e, CI=CI, cv=cv, t0=t0: e.dma_start(out=CI[:], in_=cv[:, :, PAD + t0 - 15:PAD + t0 + T + 15]), bCI, writes=[bCI])

        def stage1(it):
            sq, ti = divmod(it, tiles_per_seq)
            t0 = ti * T
            own = is_own(it)
            HI, bHI = hin[it % 3], bhin[it % 3]
            CI, bCI = cin[it % 3], bcin[it % 3]
            X0, bX0 = x0s[it % 2], bx0s[it % 2]
            VX, bVX = vxs[it % 2], bvxs[it % 2]

            def sconv(j, pi):
                S.op("pe", [(lambda e, j=j, pi=pi, k=k, HI=HI: e.matmul(psC[pi][:, :T], lhsT=Dsh[:, j * 3 + k, :], rhs=HI[:, j, k:k + T], start=(k == 0), stop=(k == 2)))
                            for k in range(3)], reads=[bDsh, bHI], writes=[bpsC[pi]])
            for q in range(4):
                if own:
                    pi = pc[0] % NPC
                    pc[0] += 1
                    sconv(q, pi)
                    S.op("act", lambda e, pi=pi, q=q, X0=X0: e.activation(out=X0[:, q, :], in_=psC[pi][:, :T], func=AF.Identity, bias=bsh[:, q, :]),
                         reads=[bpsC[pi], bbsh], writes=[bX0])
                pi = pc[0] % NPC
                pc[0] += 1
                sconv(4 + q, pi)
                xb = q % 2
                S.op("act", lambda e, pi=pi, q=q, xb=xb: e.activation(out=x1t[xb][:], in_=psC[pi][:, :T], func=AF.Identity, bias=bsh[:, 4 + q, :]),
                     reads=[bpsC[pi], bbsh], writes=[bx1t[xb]])
                pi = pc[0] % NPC
                pc[0] += 1
                sconv(8 + q, pi)
                S.op("dve", lambda e, pi=pi, q=q, xb=xb, VX=VX: e.scalar_tensor_tensor(out=VX[:, q, :], in0=psC[pi][:, :T], scalar=bsh[:, 8 + q, :], in1=x1t[xb][:],
                                                                                        op0=ALU.add, op1=ALU.mult),
                     reads=[bpsC[pi], bbsh, bx1t[xb]], writes=[bVX])
            x0v = x0_scr[sq].rearrange("(j p) t -> p j t", p=128)
            vxv = vx_scr[sq].rearrange("(j p) t -> p j t", p=128)
            if own:
                S.dma("sp", lambda e, x0v=x0v, t0=t0, X0=X0: e.dma_start(out=x0v[:, :, t0:t0 + T], in_=X0[:]), bX0, reads=[bX0])
            S.dma("sp", lambda e, vxv=vxv, t0=t0, VX=VX: e.dma_start(out=vxv[:, :, t0:t0 + T], in_=VX[:]), bVX, reads=[bVX])
            if not own:
                return
            b2 = it % 2
            for q in range(4):
                pi = pc[0] % NPC
                pc[0] += 1
                S.op("pe", [(lambda e, q=q, pi=pi, k=k, CI=CI: e.matmul(psC[pi][:, :T], lhsT=Ddw[:, q * KW + k, :], rhs=CI[:, q, k:k + T], start=(k == 0), stop=(k == KW - 1)))
                            for k in range(KW)], reads=[bDdw, bCI], writes=[bpsC[pi]])
                S.op("act", lambda e, pi=pi, q=q, b2=b2: e.activation(out=cb[b2][:, q, :], in_=psC[pi][:, :T], func=AF.Identity, bias=bdw[:, q, :]),
                     reads=[bpsC[pi], bbdw], writes=[bcb[b2][q]])
                S.op("act", lambda e, pi=pi, q=q, b2=b2: e.activation(out=csq[b2][:, q, :], in_=psC[pi][:, :T], func=AF.Square, bias=bdw[:, q, :]),
                     reads=[bpsC[pi], bbdw], writes=[bcsq[b2][q]])

        def stage1b(it):
            if not is_own(it):
                return
            b2 = it % 2
            M, bM = mean[b2], bmean[b2]
            R, bR = rstd[b2], brstd[b2]
            S.op("pe", [(lambda e, q=q, b2=b2: e.matmul(psS[b2][:, :T], lhsT=ones[:], rhs=cb[b2][:, q, :], start=(q == 0), stop=(q == 3))) for q in range(4)],
                 reads=bcb[b2] + [bones], writes=[bpsS[b2]])
            S.op("pe", [(lambda e, q=q, b2=b2: e.matmul(psQ[b2][:, :T], lhsT=ones[:], rhs=csq[b2][:, q, :], start=(q == 0), stop=(q == 3))) for q in range(4)],
                 reads=bcsq[b2] + [bones], writes=[bpsQ[b2]])
            S.op("dve", lambda e, b2=b2, M=M: e.tensor_scalar(out=M[:], in0=psS[b2][:, :T], scalar1=1.0 / DH, scalar2=None, op0=ALU.mult), reads=[bpsS[b2]], writes=[bM])
            S.op("dve", lambda e, M=M: e.tensor_tensor(out=msq[:], in0=M[:], in1=M[:], op=ALU.mult), reads=[bM], writes=[bmsq])
            S.op("dve", lambda e, b2=b2, R=R: e.scalar_tensor_tensor(out=R[:], in0=psQ[b2][:, :T], scalar=1.0 / DH, in1=msq[:], op0=ALU.mult, op1=ALU.subtract),
                 reads=[bpsQ[b2], bmsq], writes=[bR])

        def stage2(it):
            if not is_own(it):
                return
            sq, ti = divmod(it, tiles_per_seq)
            t0 = ti * T
            b2 = it % 2
            CA, bCA = cas[it % 2], bcas[it % 2]
            M, bM = mean[b2], bmean[b2]
            R, bR = rstd[b2], brstd[b2]
            S.op("act", lambda e, R=R: e.activation(out=R[:], in_=R[:], func=AF.Sqrt, bias=ceps[:, 0:1]), reads=[bR, bcc], writes=[bR])
            S.op("dve", lambda e, R=R: e.reciprocal(out=R[:], in_=R[:]), reads=[bR], writes=[bR])
            for q in range(4):
                xb = q % 2
                S.op("dve", lambda e, q=q, xb=xb, b2=b2, M=M: e.tensor_tensor(out=xc[xb][:], in0=cb[b2][:, q, :], in1=M[:], op=ALU.subtract),
                     reads=[bcb[b2][q], bM], writes=[bxc[xb]])
                S.op("dve", lambda e, xb=xb, R=R: e.tensor_tensor(out=xc[xb][:], in0=xc[xb][:], in1=R[:], op=ALU.mult),
                     reads=[bxc[xb], bR], writes=[bxc[xb]])
                S.op("act", lambda e, q=q, xb=xb, CA=CA: e.activation(out=CA[:, q, :], in_=xc[xb][:], func=AF.Silu, bias=lnb[:, q, :], scale=lng[:, q, :]),
                     reads=[bxc[xb], blng, blnb], writes=[bCA])
            cav = cact_scr[sq].rearrange("(j p) t -> p j t", p=128)
            S.dma("sp", lambda e, cav=cav, t0=t0, CA=CA: e.dma_start(out=cav[:, :, t0:t0 + T], in_=CA[:]), bCA, reads=[bCA])

        ld(0)
        if ntiles > 1:
            ld(1)
        stage1(0)
        stage1b(0)
        for it in range(ntiles):
            if it + 2 < ntiles:
                ld(it + 2)
            if it + 1 < ntiles:
                stage1(it + 1)
            stage2(it)
            if it + 1 < ntiles:
                stage1b(it + 1)
        return S.flush()


def phase_d(nc, S, x1, x2, g_pre, w_in, hy_w_out, cv_w_out, w_out, g_post, ylong_scr, x0_scr, cact_scr, tiles, T=512):
    NS = T // 128
    with contextlib.ExitStack() as st:
        sb, ps = mk_alloc(nc, st)
        Wg = sb("Wgate", [128, NKC, 2048], BF16)
        bWg = S.buf("Wgate", True)
        w_v = w_in.rearrange("(kc p) f -> p kc f", p=128)
        S.dma("pool", [(lambda e, kc=kc: e.dma_start(out=Wg[:, kc, :], in_=w_v[:, kc, 2560:4608])) for kc in range(NKC)], bWg, writes=[bWg])
        Why = sb("Why", [128, 4, D], BF16)
        Wcv = sb("Wcv", [128, 4, D], BF16)
        Wo = sb("Wo", [128, NKC, D], BF16)
        bWs = S.buf("Wsmall", True)
        S.dma("pool", [lambda e: e.dma_start(out=Why[:], in_=hy_w_out.rearrange("(q p) f -> p q f", p=128)),
                       lambda e: e.dma_start(out=Wcv[:], in_=cv_w_out.rearrange("(q p) f -> p q f", p=128)),
                       lambda e: e.dma_start(out=Wo[:], in_=w_out.rearrange("(q p) f -> p q f", p=128))], bWs, writes=[bWs])
        gpost, bgpost = load_bcast_gain(nc, S, st, "gpost", g_post, 32.0)
        bgpost.const = True
        bWg.const = False
        scale_weight_rows(nc, S, st, Wg, bWg, g_pre, 32.0, 2048)
        bWg.const = True
        nf = NormFront(nc, S, st, T, n_uT=2)
        NB = 3
        NB2 = 2
        xt = [sb("xt%d" % i, [128, NS, D], F32) for i in range(NB)]
        bxt = [S.buf("xt%d" % i) for i in range(NB)]
        ylT = [sb("ylT%d" % i, [128, 4, T], BF16) for i in range(NB2)]
        bylT = [S.buf("ylT%d" % i) for i in range(NB2)]
        x0T = [sb("x0T%d" % i, [128, 4, T], BF16) for i in range(NB2)]
        bx0T = [S.buf("x0T%d" % i) for i in range(NB2)]
        caT = [sb("caT%d" % i, [128, 4, T], BF16) for i in range(NB2)]
        bcaT = [S.buf("caT%d" % i) for i in range(NB2)]
        hT = [sb("hT%d" % i, [128, 4, T], BF16) for i in range(NB2)]
        bhT = [S.buf("hT%d" % i) for i in range(NB2)]
        sga = [sb("sga%d" % i, [128, T], BF16) for i in range(2)]
        bsga = [S.buf("sga%d" % i) for i in range(2)]
        sgb = [sb("sgb%d" % i, [128, T], BF16) for i in range(2)]
        bsgb = [S.buf("sgb%d" % i) for i in range(2)]
        m1 = [sb("m1%d" % i, [128, T], F32) for i in range(2)]
        bm1 = [S.buf("m1%d" % i) for i in range(2)]
        m2 = [sb("m2%d" % i, [128, T], F32) for i in range(2)]
        bm2 = [S.buf("m2%d" % i) for i in range(2)]
        mT = sb("mT", [128, NKC, T], BF16)
        bmT = [S.buf("mT%d" % i) for i in range(NKC)]
        tmp = sb("tmp", [128, D], F32)
        btmp = [S.buf("tmp0"), S.buf("tmp1")]
        psGa = ps("psGa", [128, 512], F32)
        psGb = ps("psGb", [128, 512], F32)
        psA = ps("psA", [128, 512], F32)
        psB = ps("psB", [128, 512], F32)
        bpsGa, bpsGb, bpsA, bpsB = [S.buf(n, excl=True) for n in ("psGa", "psGb", "psA", "psB")]
        psM = [ps("psM%d" % i, [128, 512], F32) for i in range(2)]
        bpsM = [S.buf("psM%d" % i, excl=True) for i in range(2)]
        x_v = x1.rearrange("(n s p) d -> n p s d", p=128, s=NS)
        x2_v = x2.rearrange("(n s p) d -> n p s d", p=128, s=NS)
        tps = L // T
        n = len(tiles)

        def ld(i):
            sq, ti = tiles[i]
            row = sq * tps + ti
            t0 = ti * T
            k = i % NB2
            X, bX = xt[i % NB], bxt[i % NB]
            S.dma("act", lambda e, X=X, row=row: e.dma_start(out=X[:], in_=x_v[row]), bX, writes=[bX])
            for (dst, bdst, scr) in ((ylT[k], bylT[k], ylong_scr), (x0T[k], bx0T[k], x0_scr), (caT[k], bcaT[k], cact_scr)):
                v = scr[sq].rearrange("(j p) t -> p j t", p=128)
                S.dma("act", lambda e, dst=dst, v=v, t0=t0: e.dma_start(out=dst[:], in_=v[:, :, t0:t0 + T]), bdst, writes=[bdst])

        def load(i):
            sq, ti = tiles[i]
            row = sq * tps + ti
            t0 = ti * T
            k = i % NB2
            X, bX = xt[i % NB], bxt[i % NB]
            nf.front_a(X, bX, i)
            S.op("dve", lambda e, k=k: e.tensor_tensor(out=hT[k][:], in0=ylT[k][:], in1=x0T[k][:], op=ALU.mult),
                 reads=[bylT[k], bx0T[k]], writes=[bhT[k]])

        ld(0)
        load(0)
        cur = nf.front_b(0)
        for i in range(n):
            sq, ti = tiles[i]
            row = sq * tps + ti
            uT, buT = cur
            if i + 1 < n:
                ld(i + 1)
            X, bX = xt[i % NB], bxt[i % NB]
            H, bH = hT[i % NB2], bhT[i % NB2]
            CA, bCA = caT[i % NB2], bcaT[i % NB2]
            for dj in range(NKC):
                gb = dj % 2
                S.op("pe", [(lambda e, dj=dj, kc=kc, uT=uT: e.matmul(psGa[:, :T], lhsT=Wg[:, kc, dj * 128:(dj + 1) * 128], rhs=uT[:, kc, :],
                                                                     start=(kc == 0), stop=(kc == NKC - 1))) for kc in range(NKC)],
                     reads=[bWg, buT], writes=[bpsGa])
                S.op("pe", [(lambda e, dj=dj, kc=kc, uT=uT: e.matmul(psGb[:, :T], lhsT=Wg[:, kc, 1024 + dj * 128:1024 + (dj + 1) * 128], rhs=uT[:, kc, :],
                                                                     start=(kc == 0), stop=(kc == NKC - 1))) for kc in range(NKC)],
                     reads=[bWg, buT], writes=[bpsGb])
                S.op("pe", [(lambda e, dj=dj, q=q, H=H: e.matmul(psA[:, :T], lhsT=Why[:, q, dj * 128:(dj + 1) * 128], rhs=H[:, q, :], start=(q == 0), stop=(q == 3)))
                            for q in range(4)], reads=[bWs, bH], writes=[bpsA])
                S.op("pe", [(lambda e, dj=dj, q=q, CA=CA: e.matmul(psB[:, :T], lhsT=Wcv[:, q, dj * 128:(dj + 1) * 128], rhs=CA[:, q, :], start=(q == 0), stop=(q == 3)))
                            for q in range(4)], reads=[bWs, bCA], writes=[bpsB])
                S.op("act", lambda e, gb=gb: e.activation(out=sga[gb][:], in_=psGa[:, :T], func=AF.Sigmoid), reads=[bpsGa], writes=[bsga[gb]])
                S.op("act", lambda e, gb=gb: e.activation(out=sgb[gb][:], in_=psGb[:, :T], func=AF.Sigmoid), reads=[bpsGb], writes=[bsgb[gb]])
                S.op("dve", lambda e, gb=gb: e.tensor_tensor(out=m1[gb][:], in0=psA[:, :T], in1=sga[gb][:], op=ALU.mult), reads=[bpsA, bsga[gb]], writes=[bm1[gb]])
                S.op("dve", lambda e, gb=gb: e.tensor_tensor(out=m2[gb][:], in0=psB[:, :T], in1=sgb[gb][:], op=ALU.mult), reads=[bpsB, bsgb[gb]], writes=[bm2[gb]])
                S.op("dve", lambda e, gb=gb, dj=dj: e.tensor_tensor(out=mT[:, dj, :], in0=m1[gb][:], in1=m2[gb][:], op=ALU.add),
                     reads=[bm1[gb], bm2[gb]], writes=[bmT[dj]])
                if dj == 0 and i + 1 < n:
                    load(i + 1)
                if dj in (2, 3, 4, 5) and i + 1 < n:
                    cur = nf.front_b_s(i + 1, dj - 2)
            for s in range(NS):
                for h in range(2):
                    S.op("pe", [(lambda e, s=s, h=h, dj=dj: e.matmul(psM[h][:], lhsT=mT[:, dj, s * 128:(s + 1) * 128],
                                                                     rhs=Wo[:, dj, h * 512:(h + 1) * 512], start=(dj == 0), stop=(dj == NKC - 1)))
                                for dj in range(NKC)], reads=bmT + [bWs], writes=[bpsM[h]])
                    nf.post_half(psM[h], bpsM[h], gpost, bgpost, tmp, btmp, s, i, h)
                nf.post_fin(tmp, btmp, X, bX, s, i)
            S.dma("sp", lambda e, X=X, row=row: e.dma_start(out=x2_v[row], in_=X[:]), bX, reads=[bX])
        return S.flush()

import contextlib
import numpy as np


def ffn_phase(nc, S, x_src, x_dst, g_pre, wg, wu, wd, g_post, NT, T=256):
    assert NT % T == 0 and T % 128 == 0
    NS = T // 128
    ntiles = NT // T
    with contextlib.ExitStack() as st:
        sb, ps = mk_alloc(nc, st)
        Wg = sb("Wg", [128, NKC, DFF], BF16)
        Wu = sb("Wu", [128, NKC, DFF], BF16)
        Wd = sb("Wd", [128, NJ, D], BF16)
        bWg, bWu, bWd = S.buf("Wg", True), S.buf("Wu", True), S.buf("Wd", True)
        wg_v = wg.rearrange("(kc p) f -> p kc f", p=128)
        wu_v = wu.rearrange("(kc p) f -> p kc f", p=128)
        wd_v = wd.rearrange("(j p) f -> p j f", p=128)
        S.dma("pool", [(lambda e, kc=kc: e.dma_start(out=Wg[:, kc, :], in_=wg_v[:, kc, :])) for kc in range(NKC)], bWg, writes=[bWg])
        S.dma("pool", [(lambda e, kc=kc: e.dma_start(out=Wu[:, kc, :], in_=wu_v[:, kc, :])) for kc in range(NKC)], bWu, writes=[bWu])
        S.dma("pool", [(lambda e, j=j: e.dma_start(out=Wd[:, 2 * j:2 * j + 2, :], in_=wd_v[:, 2 * j:2 * j + 2, :])) for j in range(NJ // 2)], bWd, writes=[bWd])
        gpost, bgpost = load_bcast_gain(nc, S, st, "gpost", g_post, 16.0)
        bgpost.const = True
        bWg.const = False
        bWu.const = False
        scale_weight_rows(nc, S, st, Wg, bWg, g_pre, 32.0, DFF)
        scale_weight_rows(nc, S, st, Wu, bWu, g_pre, 32.0, DFF)
        bWg.const = True
        bWu.const = True
        nf = NormFront(nc, S, st, T, n_uT=2)
        NB = 3
        xt = [sb("xt%d" % i, [128, NS, D], F32) for i in range(NB)]
        bxt = [S.buf("xt%d" % i) for i in range(NB)]
        aT = sb("aT", [128, NJ, T], BF16)
        baT = [S.buf("aT%d" % j) for j in range(NJ)]
        sg = [sb("sg%d" % i, [128, T], F32) for i in range(2)]
        bsg = [S.buf("sg%d" % i) for i in range(2)]
        tmp = sb("tmp", [128, D], F32)
        btmp = [S.buf("tmp0"), S.buf("tmp1")]
        psG = [ps("psG%d" % i, [128, 512], F32) for i in range(2)]
        psU = [ps("psU%d" % i, [128, 512], F32) for i in range(2)]
        bpsG = [S.buf("psG%d" % i, excl=True) for i in range(2)]
        bpsU = [S.buf("psU%d" % i, excl=True) for i in range(2)]
        psD = [ps("psD%d" % i, [128, 512], F32) for i in range(2)]
        bpsD = [S.buf("psD%d" % i, excl=True) for i in range(2)]
        x_src_v = x_src.rearrange("(n s p) d -> n p s d", p=128, s=NS)
        x_dst_v = x_dst.rearrange("(n s p) d -> n p s d", p=128, s=NS)
        gc = [0]

        def ld(i):
            X, bX = xt[i % NB], bxt[i % NB]
            S.dma("act", lambda e, X=X, i=i: e.dma_start(out=X[:], in_=x_src_v[i]), bX, writes=[bX])

        def load(i):
            nf.front_a(xt[i % NB], bxt[i % NB], i)

        def gu(i, j, uT, buT):
            gb = gc[0] % 2
            gc[0] += 1
            S.op("pe", [(lambda e, gb=gb, j=j, kc=kc: e.matmul(psG[gb][:, :T], lhsT=Wg[:, kc, j * 128:(j + 1) * 128], rhs=uT[:, kc, :],
                                                               start=(kc == 0), stop=(kc == NKC - 1))) for kc in range(NKC)],
                 reads=[bWg, buT], writes=[bpsG[gb]])
            S.op("pe", [(lambda e, gb=gb, j=j, kc=kc: e.matmul(psU[gb][:, :T], lhsT=Wu[:, kc, j * 128:(j + 1) * 128], rhs=uT[:, kc, :],
                                                               start=(kc == 0), stop=(kc == NKC - 1))) for kc in range(NKC)],
                 reads=[bWu, buT], writes=[bpsU[gb]])
            S.op("act", lambda e, gb=gb: e.activation(out=sg[gb][:], in_=psG[gb][:, :T], func=AF.Silu), reads=[bpsG[gb]], writes=[bsg[gb]])
            S.op("dve", lambda e, gb=gb, j=j: e.tensor_tensor(out=aT[:, j, :], in0=psU[gb][:, :T], in1=sg[gb][:], op=ALU.mult),
                 reads=[bpsU[gb], bsg[gb]], writes=[baT[j]])

        def down(i, s):
            X, bX = xt[i % NB], bxt[i % NB]
            for h in range(2):
                S.op("pe", [(lambda e, s=s, h=h, j=j: e.matmul(psD[h][:], lhsT=aT[:, j, s * 128:(s + 1) * 128],
                                                               rhs=Wd[:, j, h * 512:(h + 1) * 512], start=(j == 0), stop=(j == NJ - 1)))
                            for j in range(NJ)], reads=baT + [bWd], writes=[bpsD[h]])
                nf.post_half(psD[h], bpsD[h], gpost, bgpost, tmp, btmp, s, i, h)
            nf.post_fin(tmp, btmp, X, bX, s, i)

        ld(0)
        if ntiles > 1:
            ld(1)
        load(0)
        cur = nf.front_b(0)
        for i in range(ntiles):
            uT, buT = cur
            if i + 2 < ntiles:
                ld(i + 2)
            for j in range(NJ):
                gu(i, j, uT, buT)
                if j == 4 and i + 1 < ntiles:
                    load(i + 1)
                if j in (10, 15) and i + 1 < ntiles:
                    cur = nf.front_b_s(i + 1, (j - 10) // 5)
            for s in range(NS):
                down(i, s)
            X, bX = xt[i % NB], bxt[i % NB]
            S.dma("sp", lambda e, X=X, i=i: e.dma_start(out=x_dst_v[i], in_=X[:]), bX, reads=[bX])
        return S.flush()

import contextlib
import math
import numpy as np
import ml_dtypes

NFFT = 2 * L
CB = 32
NBLK = 512 // CB
FH = 64


def host_consts():
    bf = ml_dtypes.bfloat16
    a = np.arange(128)
    c = {}
    ang = -2 * np.pi * np.outer(a, a) / 128.0
    Er, Ei = np.cos(ang), np.sin(ang)
    c["F1d"] = np.concatenate([np.concatenate([Er[:64], Ei[:64]], 1), np.concatenate([-Ei[:64], Er[:64]], 1)], 0).astype(bf)
    c["F1k"] = np.concatenate([Er, Ei], 1).astype(bf)
    k = a[None, :, None] + 128 * a[None, None, :]
    angg = -2 * np.pi * (a[:, None, None] * k) / NFFT
    c["Gr"] = np.cos(angg).astype(bf)
    c["Gi"] = np.sin(angg).astype(bf)
    angh = 2 * np.pi * np.outer(a, a) / 128.0
    Hr, Hi = np.cos(angh), np.sin(angh)
    c["H1"] = np.concatenate([Hr, Hi], 1).astype(bf)
    c["H2"] = np.concatenate([-Hi, Hr], 1).astype(bf)
    th = np.arange(64)
    t = a[None, :, None] + 128 * th[None, None, :]
    angp = 2 * np.pi * (a[:, None, None] * t) / NFFT
    Gpr, Gpi = np.cos(angp), np.sin(angp)
    c["T1"] = np.concatenate([Gpr, Gpi], 2).astype(bf)
    c["T2"] = np.concatenate([-Gpi, Gpr], 2).astype(bf)
    n = np.arange(NFFT)
    j = np.where(n <= L, n, NFFT - n).astype(np.float64)
    j[L] = 0
    tj = j / (L - 1)
    wj = 2 * np.pi * j / L
    bands = np.linspace(1e-4, 15.0, 16)
    angf = wj[:, None] * bands[None, :]
    z = np.concatenate([tj[:, None], np.cos(angf), -np.sin(angf)], 1)
    c["zT"] = np.ascontiguousarray(z.T).astype(np.float32)
    tneg = -tj.copy()
    tneg[L] = -1e4
    c["tneg"] = np.ascontiguousarray(tneg.reshape(128, 128)).astype(np.float32)
    max_decay = math.log(1e-2) / 0.3
    min_decay = math.log(1e-2) / 1.5
    c["absdelta"] = np.abs(np.linspace(min_decay, max_decay, 512)).astype(np.float32)
    return c


CONST_SPECS = {"F1d": ([128, 256], BF16), "F1k": ([128, 256], BF16), "Gr": ([128, 128, 128], BF16), "Gi": ([128, 128, 128], BF16),
               "H1": ([128, 256], BF16), "H2": ([128, 256], BF16), "T1": ([128, 128, 128], BF16), "T2": ([128, 128, 128], BF16),
               "zT": ([33, NFFT], F32), "tneg": ([128, 128], F32), "absdelta": ([512], F32)}


def load_const(nc, S, st, name, dram, shape, dt, queue="sp"):
    t = st.enter_context(nc.sbuf_tensor(uname(name), shape, dt))
    b = S.buf(name, True)
    S.dma(queue, lambda e: e.dma_start(out=t[:], in_=dram), b, writes=[b])
    return t, b


class Fwd:
    def __init__(self, nc, S, st, cd, f1name):
        sb, ps = mk_alloc(nc, st)
        self.S = S
        self.F1, self.bF1 = load_const(nc, S, st, f1name, cd[f1name], [128, 256], BF16)
        self.Gr, self.bGr = load_const(nc, S, st, "Gr", cd["Gr"], [128, 128, 128], BF16)
        self.Gi, self.bGi = load_const(nc, S, st, "Gi", cd["Gi"], [128, 128, 128], BF16)
        self.A2 = [sb("A%d" % i, [128, CB, 3, 128], BF16) for i in range(2)]
        self.bA2 = [S.buf("A%d" % i) for i in range(2)]
        self.Xs2 = [sb("Xs%d" % i, [128, 2, 128, CB], BF16) for i in range(2)]
        self.bXs2 = [S.buf("Xs%d" % i) for i in range(2)]
        self.xcnt = 0
        self.ps1 = [ps("psF1_%d" % i, [128, 2, 256], F32) for i in range(2)]
        self.bps1 = [S.buf("psF1_%d" % i, excl=True) for i in range(2)]
        self.ps2 = [ps("psF2_%d" % i, [128, 8, 2, CB], F32) for i in range(2)]
        self.bps2 = [S.buf("psF2_%d" % i, excl=True) for i in range(2)]
        self.cnt1 = 0
        self.cnt2 = 0

    def f1_pair(self, Zin, bZin, slot, c2):
        S = self.S
        A, bA = self.A2[slot % 2], self.bA2[slot % 2]
        pi = self.cnt1 % 2
        self.cnt1 += 1
        P, bP = self.ps1[pi], self.bps1[pi]
        S.op("pe", [(lambda e, P=P, c2=c2, i=i: e.matmul(P[:, i, :], lhsT=Zin[:, 2 * c2 + i, :], rhs=self.F1[:], start=True, stop=True)) for i in range(2)],
             reads=[bZin, self.bF1], writes=[bP])
        S.op("act", lambda e, P=P, c2=c2, A=A: e.copy(out=A[:, 2 * c2:2 * c2 + 2, 0:2, :], in_=P.rearrange("p c (r k) -> p c r k", r=2)),
             reads=[bP], writes=[bA])
        S.op("act", lambda e, P=P, c2=c2, A=A: e.activation(out=A[:, 2 * c2:2 * c2 + 2, 2, :], in_=P[:, :, 128:256], func=AF.Copy, scale=-1.0),
             reads=[bP], writes=[bA])

    def f2_group(self, slot, g):
        S = self.S
        A, bA = self.A2[slot % 2], self.bA2[slot % 2]
        Xs, bXs = self.Xs2[slot % 2], self.bXs2[slot % 2]
        pi = self.cnt2 % 2
        self.cnt2 += 1
        P, bP = self.ps2[pi], self.bps2[pi]
        fns = []
        for kk in range(8):
            ka = g * 8 + kk
            fns.append(lambda e, P=P, kk=kk, ka=ka, A=A: e.matmul(P[:, kk, 0, :], lhsT=self.Gr[:, ka, :], rhs=A[:, :, 0, ka], start=True, stop=False))
            fns.append(lambda e, P=P, kk=kk, ka=ka, A=A: e.matmul(P[:, kk, 0, :], lhsT=self.Gi[:, ka, :], rhs=A[:, :, 2, ka], start=False, stop=True))
            fns.append(lambda e, P=P, kk=kk, ka=ka, A=A: e.matmul(P[:, kk, 1, :], lhsT=self.Gi[:, ka, :], rhs=A[:, :, 0, ka], start=True, stop=False))
            fns.append(lambda e, P=P, kk=kk, ka=ka, A=A: e.matmul(P[:, kk, 1, :], lhsT=self.Gr[:, ka, :], rhs=A[:, :, 1, ka], start=False, stop=True))
        S.op("pe", fns, reads=[bA, self.bGr, self.bGi], writes=[bP])
        S.op("act", lambda e, P=P, g=g, Xs=Xs: e.copy(out=Xs[:, :, g * 8:(g + 1) * 8, :], in_=P.rearrange("p k r c -> p r k c")), reads=[bP], writes=[bXs])

    def pipeline(self, nblk, get_zin, on_block_start, on_block_done):
        on_block_start(0)
        Z, bZ = get_zin(0)
        for c2 in range(CB // 2):
            self.f1_pair(Z, bZ, 0, c2)
        for blk in range(nblk):
            if blk + 1 < nblk:
                on_block_start(blk + 1)
                Z, bZ = get_zin(blk + 1)
            for g in range(16):
                self.f2_group(blk, g)
                if blk + 1 < nblk:
                    self.f1_pair(Z, bZ, blk + 1, g)
            self.Xs, self.bXs = self.Xs2[blk % 2], self.bXs2[blk % 2]
            on_block_done(blk)


def phase_c0(nc, S, cd, w1, b1, f1, w2, b2, f2, w3, hy_bias, kt_scr, kspec):
    TWO_PI = 2 * math.pi
    with contextlib.ExitStack() as st:
        sb, ps = mk_alloc(nc, st)
        w1s, bw1 = load_const(nc, S, st, "w1s", w1, [33, FH], F32)
        w2s, bw2 = load_const(nc, S, st, "w2s", w2, [FH, FH], F32)
        w3s = sb("w3s", [FH, 1024], BF16)
        bw3 = S.buf("w3s", True)
        S.dma("pool", lambda e: e.dma_start(out=w3s[:], in_=w3), bw3, writes=[bw3])
        w3sum = sb("w3sum", [FH, 512], BF16)
        bw3sum = S.buf("w3sum", True)
        S.op("dve", lambda e: e.tensor_tensor(out=w3sum[:], in0=w3s[:, 0:512], in1=w3s[:, 512:1024], op=ALU.add), reads=[bw3], writes=[bw3sum])
        col = sb("fcol", [FH, 8], F32)
        bcol = S.buf("fcol", True)
        S.dma("sp", [lambda e: e.dma_start(out=col[:, 0:1], in_=b1.rearrange("(p o) -> p o", o=1)),
                     lambda e: e.dma_start(out=col[:, 1:2], in_=f1.rearrange("(p o) -> p o", o=1)),
                     lambda e: e.dma_start(out=col[:, 2:3], in_=b2.rearrange("(p o) -> p o", o=1)),
                     lambda e: e.dma_start(out=col[:, 3:4], in_=f2.rearrange("(p o) -> p o", o=1))], bcol, writes=[bcol])
        S.op("dve", lambda e: e.tensor_tensor(out=col[:, 4:5], in0=col[:, 0:1], in1=col[:, 1:2], op=ALU.mult), reads=[bcol], writes=[bcol])
        S.op("dve", lambda e: e.tensor_tensor(out=col[:, 5:6], in0=col[:, 2:3], in1=col[:, 3:4], op=ALU.mult), reads=[bcol], writes=[bcol])
        S.op("pool", lambda e: e.memset(col[:, 6:7], -math.pi), reads=[bcol], writes=[bcol])
        h2T = sb("h2T", [FH, NFFT], BF16)
        bh2T = S.buf("h2T")
        st2 = contextlib.ExitStack()
        sb2, ps2 = mk_alloc(nc, st2)
        rt = [sb2("rrt%d" % i, [FH, 512], F32) for i in range(4)]
        brt = [S.buf("rrt%d" % i) for i in range(4)]
        rrc = [0]
        MAGIC = 12582912.0

        def rr(a, ba):
            i = rrc[0] % 4
            rrc[0] += 1
            t, bt = rt[i], brt[i]
            S.op("dve", lambda e: e.tensor_scalar(out=t[:], in0=a[:], scalar1=1.0 / TWO_PI, scalar2=MAGIC, op0=ALU.mult, op1=ALU.add), reads=[ba], writes=[bt])
            S.op("dve", lambda e: e.tensor_scalar(out=t[:], in0=t[:], scalar1=MAGIC, scalar2=TWO_PI, op0=ALU.subtract, op1=ALU.mult), reads=[bt], writes=[bt])
            S.op("dve", lambda e: e.tensor_tensor(out=a[:], in0=a[:], in1=t[:], op=ALU.subtract), reads=[ba, bt], writes=[ba])
            S.op("dve", lambda e: e.tensor_scalar(out=a[:], in0=a[:], scalar1=3.1415925, scalar2=-3.1415925, op0=ALU.min, op1=ALU.max), reads=[ba], writes=[ba])
        NMB = 4
        zc = [sb2("zc%d" % i, [33, 512], F32) for i in range(NMB)]
        bzc = [S.buf("zc%d" % i) for i in range(NMB)]
        a1 = [sb2("a1_%d" % i, [FH, 512], F32) for i in range(NMB)]
        ba1 = [S.buf("a1_%d" % i) for i in range(NMB)]
        h1 = [sb2("h1_%d" % i, [FH, 512], F32) for i in range(NMB)]
        bh1 = [S.buf("h1_%d" % i) for i in range(NMB)]
        a2 = [sb2("a2_%d" % i, [FH, 512], F32) for i in range(NMB)]
        ba2 = [S.buf("a2_%d" % i) for i in range(NMB)]
        psm = [ps2("psm%d" % i, [FH, 512], F32) for i in range(2 * NMB)]
        bpsm = [S.buf("psm%d" % i, excl=True) for i in range(2 * NMB)]
        zT = cd["zT"]
        for ci in range(NFFT // 512):
            b = ci % NMB
            n0 = ci * 512
            S.dma("sp", lambda e, b=b, n0=n0: e.dma_start(out=zc[b][:], in_=zT[:, n0:n0 + 512]), bzc[b], writes=[bzc[b]])
            p1, p2 = psm[2 * b], psm[2 * b + 1]
            bp1, bp2 = bpsm[2 * b], bpsm[2 * b + 1]
            S.op("pe", lambda e, b=b, p1=p1: e.matmul(p1[:], lhsT=w1s[:], rhs=zc[b][:], start=True, stop=True), reads=[bw1, bzc[b]], writes=[bp1])
            S.op("act", lambda e, b=b, p1=p1: e.activation(out=a1[b][:], in_=p1[:], func=AF.Identity, bias=col[:, 4:5], scale=col[:, 1:2]),
                 reads=[bp1, bcol], writes=[ba1[b]])
            rr(a1[b], ba1[b])
            S.op("act", lambda e, b=b: e.activation(out=h1[b][:], in_=a1[b][:], func=AF.Sin), reads=[ba1[b]], writes=[bh1[b]])
            S.op("pe", lambda e, b=b, p2=p2: e.matmul(p2[:], lhsT=w2s[:], rhs=h1[b][:], start=True, stop=True), reads=[bw2, bh1[b]], writes=[bp2])
            S.op("act", lambda e, b=b, p2=p2: e.activation(out=a2[b][:], in_=p2[:], func=AF.Identity, bias=col[:, 5:6], scale=col[:, 3:4]),
                 reads=[bp2, bcol], writes=[ba2[b]])
            rr(a2[b], ba2[b])
            S.op("act", lambda e, b=b, n0=n0: e.activation(out=h2T[:, n0:n0 + 512], in_=a2[b][:], func=AF.Sin),
                 reads=[ba2[b]], writes=[bh2T])
        S.flush(keep=True)
        st2.close()
        kt = sb("kt", [128, 512, 128], BF16)
        bkt = S.buf("kt")
        tneg, btneg = load_const(nc, S, st, "tneg", cd["tneg"], [128, 128], F32)
        adl = sb("adl", [128, 512], F32)
        badl = S.buf("adl", True)
        S.dma("sp", lambda e: e.dma_start(out=adl[:], in_=cd["absdelta"].partition_broadcast(128)), badl, writes=[badl])
        kacc = sb("kacc", [128, 512], F32)
        bkacc = S.buf("kacc")
        S.op("pool", lambda e: e.memset(kacc[:], 0.0), writes=[bkacc])
        NKB = 4
        Ex = [sb("Ex%d" % i, [128, 512], F32) for i in range(NKB)]
        bEx = [S.buf("Ex%d" % i) for i in range(NKB)]
        kf = [sb("kf%d" % i, [128, 512], F32) for i in range(NKB)]
        bkf = [S.buf("kf%d" % i) for i in range(NKB)]
        sq = [sb("sq%d" % i, [128, 512], F32) for i in range(NKB)]
        bsq = [S.buf("sq%d" % i) for i in range(NKB)]
        psK = [ps("psK%d" % i, [128, 512], F32) for i in range(NKB)]
        bpsK = [S.buf("psK%d" % i, excl=True) for i in range(NKB)]
        h3 = h2T.rearrange("p (a b) -> p a b", b=128)
        for tl in range(128):
            b = tl % NKB
            P, bP = psK[b], bpsK[b]
            fns = [lambda e, P=P, tl=tl: e.matmul(P[0:64, :], lhsT=h3[:, 0:64, tl], rhs=w3s[:, 0:512], start=True, stop=True),
                   lambda e, P=P, tl=tl: e.matmul(P[64:128, :], lhsT=h3[:, 64:128, tl], rhs=w3s[:, 512:1024], start=True, stop=True, tile_position=(0, 64))]
            if tl == 0:
                fns.append(lambda e, P=P: e.matmul(P[0:1, :], lhsT=h2T[:, 0:1], rhs=w3sum[:], start=True, stop=True))
            S.op("pe", fns, reads=[bh2T, bw3, bw3sum], writes=[bP])
            S.op("act", lambda e, b=b, tl=tl: e.activation(out=Ex[b][:], in_=adl[:], func=AF.Exp, scale=tneg[:, tl:tl + 1]),
                 reads=[badl, btneg], writes=[bEx[b]])
            S.op("dve", lambda e, b=b, P=P: e.tensor_tensor(out=kf[b][:], in0=P[:], in1=Ex[b][:], op=ALU.mult), reads=[bP, bEx[b]], writes=[bkf[b]])
            if tl % 2 == 0:
                S.op("act", lambda e, b=b, tl=tl: e.copy(out=kt[:, :, tl], in_=kf[b][:]), reads=[bkf[b]], writes=[bkt])
            else:
                S.op("pool", lambda e, b=b, tl=tl: e.tensor_copy(out=kt[:, :, tl], in_=kf[b][:]), reads=[bkf[b]], writes=[bkt])
            S.op("dve", lambda e, b=b: e.tensor_tensor(out=sq[b][:], in0=kf[b][:], in1=kf[b][:], op=ALU.mult), reads=[bkf[b]], writes=[bsq[b]])
            S.op("dve", lambda e, b=b: e.tensor_tensor(out=kacc[:], in0=kacc[:], in1=sq[b][:], op=ALU.add), reads=[bkacc, bsq[b]], writes=[bkacc])
        S.dma("sp", lambda e: e.dma_start(out=kt_scr, in_=kt[:]), bkt, reads=[bkt])
        onesf = sb("onesf", [128, 128], F32)
        bones = S.buf("onesf", True)
        S.op("pool", lambda e: e.memset(onesf[:], 1.0), writes=[bones])
        S.op("pe", lambda e: e.matmul(psK[0][:], lhsT=onesf[:], rhs=kacc[:], start=True, stop=True), reads=[bones, bkacc], writes=[bpsK[0]])
        rn = sb("rn", [128, 512], F32)
        brn = S.buf("rn")
        cE = sb("cE", [128, 512], F32)
        cM = sb("cM", [128, 512], F32)
        bcE = S.buf("cE", True)
        S.op("pool", [lambda e: e.memset(cE[:], EPS), lambda e: e.memset(cM[:], -0.5)], writes=[bcE])
        S.op("dve", lambda e: e.tensor_copy(out=rn[:], in_=psK[0][:]), reads=[bpsK[0]], writes=[brn])
        S.op("act", lambda e: e.activation(out=rn[:], in_=rn[:], func=AF.Sqrt, bias=cE[:, 0:1]), reads=[brn, bcE], writes=[brn])
        S.op("dve", lambda e: e.reciprocal(out=rn[:], in_=rn[:]), reads=[brn], writes=[brn])
        S.op("act", lambda e: e.mul(rn[:], rn[:], 1.0 / NFFT), reads=[brn], writes=[brn])
        S.dma("sp", lambda e: e.dma_start(out=cd["rn_scr"], in_=rn[0:1, :]), brn, reads=[brn])
        S.flush()
    with contextlib.ExitStack() as st:
        sb, ps = mk_alloc(nc, st)
        fw = Fwd(nc, S, st, cd, "F1k")
        rn = sb("rn2", [128, 512], F32)
        brn = S.buf("rn2", True)
        S.dma("sp", lambda e: e.dma_start(out=rn[:], in_=cd["rn_scr"].rearrange("o c -> (o c)").partition_broadcast(128)), brn, writes=[brn])
        hb = sb("hb", [128, 512], F32)
        bhb = S.buf("hb", True)
        S.dma("sp", lambda e: e.dma_start(out=hb[:], in_=hy_bias.partition_broadcast(128)), bhb, writes=[bhb])
        S.op("act", lambda e: e.mul(hb[:], hb[:], 1.0 / NFFT), reads=[bhb], writes=[bhb])
        Zin = [sb("Zin%d" % i, [128, CB, 128], BF16) for i in range(2)]
        bZin = [S.buf("Zin%d" % i) for i in range(2)]
        Kt = [sb("Kt%d" % i, [128, 2, 128, CB], BF16) for i in range(2)]
        bKt = [S.buf("Kt%d" % i) for i in range(2)]
        ktv = kt_scr.rearrange("p (c b) -> p c b", b=128)
        def ldz(blk):
            b = blk % 2
            c0 = blk * CB
            S.dma("act", lambda e, b=b, c0=c0: e.dma_start(out=Zin[b][:], in_=ktv[:, c0:c0 + CB, :]), bZin[b], writes=[bZin[b]])

        def done(blk):
            b = blk % 2
            c0 = blk * CB
            rnb = rn[:, c0:c0 + CB].unsqueeze(1).unsqueeze(1).broadcast_to([128, 2, 128, CB])
            hbb = hb[:, c0:c0 + CB].unsqueeze(1).broadcast_to([128, 128, CB])
            XS = fw.Xs
            S.op("dve", lambda e, b=b, rnb=rnb, XS=XS: e.tensor_tensor(out=Kt[b][:], in0=XS[:], in1=rnb, op=ALU.mult), reads=[fw.bXs, brn], writes=[bKt[b]])
            S.op("dve", lambda e, b=b, hbb=hbb: e.tensor_tensor(out=Kt[b][:, 0, :, :], in0=Kt[b][:, 0, :, :], in1=hbb, op=ALU.add), reads=[bKt[b], bhb], writes=[bKt[b]])
            S.dma("sp", lambda e, b=b, blk=blk: e.dma_start(out=kspec[blk], in_=Kt[b].rearrange("p r a c -> p (r a c)")), bKt[b], reads=[bKt[b]])
        fw.pipeline(NBLK, lambda blk: (Zin[blk % 2], bZin[blk % 2]), ldz, done)
        S.flush()


def phase_c1(nc, S, cd, vx_scr, kspec, yspec):
    with contextlib.ExitStack() as st:
        sb, ps = mk_alloc(nc, st)
        fw = Fwd(nc, S, st, cd, "F1d")
        Zin = [sb("Zin%d" % i, [128, CB, 128], BF16) for i in range(2)]
        bZin = [S.buf("Zin%d" % i) for i in range(2)]
        Kt = [sb("Kt0", [128, 2, 128, CB], BF16)] * 2
        bKt = [S.buf("Kt0")] * 2
        Y = [sb("Y0", [128, 2, 128, CB], BF16)] * 2
        bY = [S.buf("Y0")] * 2
        tt = [sb("tt0", [128, 128, CB], BF16)]
        btt = [S.buf("tt0")]
        def ldz(blk):
            b = blk % 2
            c0 = blk * CB
            S.dma("act", [(lambda e, b=b, c0=c0, r=r: e.dma_start(out=Zin[b][r * 64:(r + 1) * 64, :, :],
                                                                   in_=vx_scr[r][c0:c0 + CB].rearrange("c (a t) -> a c t", t=128))) for r in range(2)],
                  bZin[b], writes=[bZin[b]])

        def done(blk):
            b = blk % 2
            S.dma("sp", lambda e, b=b, blk=blk: e.dma_start(out=Kt[b].rearrange("p r a c -> p (r a c)"), in_=kspec[blk]), bKt[b], writes=[bKt[b]])
            Xre, Xim = fw.Xs[:, 0, :, :], fw.Xs[:, 1, :, :]
            Kre, Kim = Kt[b][:, 0, :, :], Kt[b][:, 1, :, :]
            Yre = Y[b][:, 0, :, :]
            Yim = Y[b][:, 1, :, :]
            S.op("dve", lambda e, Xre=Xre, Kre=Kre, Yre=Yre: e.tensor_tensor(out=Yre, in0=Xre, in1=Kre, op=ALU.mult), reads=[fw.bXs, bKt[b]], writes=[bY[b]])
            S.op("dve", lambda e, Xim=Xim, Kim=Kim: e.tensor_tensor(out=tt[0][:], in0=Xim, in1=Kim, op=ALU.mult), reads=[fw.bXs, bKt[b]], writes=[btt[0]])
            S.op("dve", lambda e, Yre=Yre: e.tensor_tensor(out=Yre, in0=Yre, in1=tt[0][:], op=ALU.subtract), reads=[btt[0], bY[b]], writes=[bY[b]])
            S.op("dve", lambda e, Xre=Xre, Kim=Kim, Yim=Yim: e.tensor_tensor(out=Yim, in0=Xre, in1=Kim, op=ALU.mult), reads=[fw.bXs, bKt[b]], writes=[bY[b]])
            S.op("dve", lambda e, Xim=Xim, Kre=Kre: e.tensor_tensor(out=tt[0][:], in0=Xim, in1=Kre, op=ALU.mult), reads=[fw.bXs, bKt[b]], writes=[btt[0]])
            S.op("dve", lambda e, Yim=Yim: e.tensor_tensor(out=Yim, in0=Yim, in1=tt[0][:], op=ALU.add), reads=[btt[0], bY[b]], writes=[bY[b]])
            S.dma("sp", lambda e, b=b, blk=blk: e.dma_start(out=yspec[blk], in_=Y[b].rearrange("p r k c -> p (r k c)")), bY[b], reads=[bY[b]])
        fw.pipeline(NBLK, lambda blk: (Zin[blk % 2], bZin[blk % 2]), ldz, done)
        S.flush()


def phase_c2(nc, S, cd, yspec, ylong_scr):
    with contextlib.ExitStack() as st:
        sb, ps = mk_alloc(nc, st)
        H1, bH1 = load_const(nc, S, st, "H1", cd["H1"], [128, 256], BF16)
        H2, bH2 = load_const(nc, S, st, "H2", cd["H2"], [128, 256], BF16)
        T1, bT1 = load_const(nc, S, st, "T1", cd["T1"], [128, 128, 128], BF16)
        T2, bT2 = load_const(nc, S, st, "T2", cd["T2"], [128, 128, 128], BF16)
        Y = [sb("Yi%d" % i, [128, 2, 128, CB], BF16) for i in range(2)]
        bY = [S.buf("Yi%d" % i) for i in range(2)]
        B = sb("B", [128, CB, 2, 128], BF16)
        bB = S.buf("B")
        Yo = [sb("Yo%d" % i, [128, CB, 128], BF16) for i in range(2)]
        bYo = [S.buf("Yo%d" % i) for i in range(2)]
        ps1 = [ps("psI1_%d" % i, [128, 2, 256], F32) for i in range(2)]
        bps1 = [S.buf("psI1_%d" % i, excl=True) for i in range(2)]
        ps2 = [ps("psI2_%d" % i, [128, 16, CB], F32) for i in range(2)]
        bps2 = [S.buf("psI2_%d" % i, excl=True) for i in range(2)]
        cnt = 0
        def ldy(blk):
            b = blk % 2
            S.dma("act", lambda e, b=b, blk=blk: e.dma_start(out=Y[b].rearrange("p r k c -> p (r k c)"), in_=yspec[blk]), bY[b], writes=[bY[b]])
        ldy(0)
        for blk in range(NBLK):
            b = blk % 2
            c0 = blk * CB
            if blk + 1 < NBLK:
                ldy(blk + 1)
            for c2 in range(CB // 2):
                pi = cnt % 2
                cnt += 1
                P, bP = ps1[pi], bps1[pi]
                fns = []
                for i in range(2):
                    c = 2 * c2 + i
                    fns.append(lambda e, P=P, i=i, c=c, b=b: e.matmul(P[:, i, :], lhsT=Y[b][:, 0, :, c], rhs=H1[:], start=True, stop=False))
                    fns.append(lambda e, P=P, i=i, c=c, b=b: e.matmul(P[:, i, :], lhsT=Y[b][:, 1, :, c], rhs=H2[:], start=False, stop=True))
                S.op("pe", fns, reads=[bY[b], bH1, bH2], writes=[bP])
                if c2 % 2 == 0:
                    S.op("act", lambda e, P=P, c2=c2: e.copy(out=B[:, 2 * c2:2 * c2 + 2, :, :], in_=P.rearrange("p c (r k) -> p c r k", r=2)), reads=[bP], writes=[bB])
                else:
                    S.op("dve", lambda e, P=P, c2=c2: e.tensor_copy(out=B[:, 2 * c2:2 * c2 + 2, :, :], in_=P.rearrange("p c (r k) -> p c r k", r=2)), reads=[bP], writes=[bB])
            for g in range(8):
                pi = cnt % 2
                cnt += 1
                P, bP = ps2[pi], bps2[pi]
                fns = []
                for tt_ in range(16):
                    tl = g * 16 + tt_
                    fns.append(lambda e, P=P, tt_=tt_, tl=tl: e.matmul(P[:, tt_, :], lhsT=T1[:, tl, :], rhs=B[:, :, 0, tl], start=True, stop=False))
                    fns.append(lambda e, P=P, tt_=tt_, tl=tl: e.matmul(P[:, tt_, :], lhsT=T2[:, tl, :], rhs=B[:, :, 1, tl], start=False, stop=True))
                S.op("pe", fns, reads=[bB, bT1, bT2], writes=[bP])
                if g % 2 == 0:
                    S.op("act", lambda e, P=P, g=g, b=b: e.copy(out=Yo[b][:, :, g * 16:(g + 1) * 16], in_=P.rearrange("p t c -> p c t")), reads=[bP], writes=[bYo[b]])
                else:
                    S.op("dve", lambda e, P=P, g=g, b=b: e.tensor_copy(out=Yo[b][:, :, g * 16:(g + 1) * 16], in_=P.rearrange("p t c -> p c t")), reads=[bP], writes=[bYo[b]])
            S.dma("sp", [(lambda e, b=b, c0=c0, r=r: e.dma_start(out=ylong_scr[r][c0:c0 + CB].rearrange("c (a t) -> a c t", t=128),
                                                                   in_=Yo[b][r * 64:(r + 1) * 64, :, :])) for r in range(2)], bYo[b], reads=[bYo[b]])
        S.flush()

from concourse.bass_utils import run_bass_kernel_spmd

NCORES = 8
NSEQ_CORE = 2
NT_CORE = NSEQ_CORE * L
W_SPECS = [
    ("ffn1_norm_pre", [D]), ("ffn1_w_gate", [D, DFF]), ("ffn1_w_up", [D, DFF]), ("ffn1_w_down", [DFF, D]), ("ffn1_norm_post", [D]),
    ("mix_norm_pre", [D]), ("w_in", [D, 4608]), ("hy_short_w", [3, 1536]), ("hy_short_b", [1536]),
    ("hy_filt_w1", [33, 64]), ("hy_filt_b1", [64]), ("hy_filt_freq1", [64]), ("hy_filt_w2", [64, 64]), ("hy_filt_b2", [64]),
    ("hy_filt_freq2", [64]), ("hy_filt_w3", [64, 1024]), ("hy_bias", [512]), ("hy_w_out", [512, D]),
    ("cv_dw_w", [31, 512]), ("cv_dw_b", [512]), ("cv_ln_g", [512]), ("cv_ln_b", [512]), ("cv_w_out", [512, D]),
    ("w_out", [D, D]), ("mix_norm_post", [D]),
    ("ffn2_norm_pre", [D]), ("ffn2_w_gate", [D, DFF]), ("ffn2_w_up", [D, DFF]), ("ffn2_w_down", [DFF, D]), ("ffn2_norm_post", [D]),
]


NT_OUT = L + L // 2
TPS = L // 512


def build_program():
    nc = bass.Bass("TRN2", target_bir_lowering=False)

    def inp(name, shape, dt=F32):
        return nc.dram_tensor(name, list(shape), dt, kind="ExternalInput").ap()

    def scr(name, shape, dt=BF16):
        return nc.dram_tensor(name, list(shape), dt, kind="Internal").ap()
    x = inp("x", [NT_CORE, D])
    y = nc.dram_tensor("y", [NT_OUT, D], F32, kind="ExternalOutput").ap()
    w = {n: inp(n, s) for n, s in W_SPECS}
    cd = {k: inp("c_" + k, *CONST_SPECS[k]) for k in CONST_SPECS}
    cd["rn_scr"] = scr("rn_scr", [1, 512], F32)
    x1 = scr("x1_scr", [NT_CORE, D], F32)
    x2 = scr("x2_scr", [NT_CORE, D], F32)
    hyp = scr("hyp_scr", [NSEQ_CORE, NHY, LP])
    cpre = scr("cpre_scr", [NSEQ_CORE, DH, LP])
    vx = scr("vx_scr", [NSEQ_CORE, DH, L])
    x0 = scr("x0_scr", [NSEQ_CORE, DH, L])
    cact = scr("cact_scr", [NSEQ_CORE, DH, L])
    ylong = scr("ylong_scr", [NSEQ_CORE, DH, L])
    kt_scr = scr("kt_scr", [128, 512 * 128])
    kspec = scr("kspec", [NBLK, 128, 128 * 2 * CB])
    yspec = scr("yspec", [NBLK, 128, 2 * CB * 128])
    own = [TPS, TPS // 2]
    with contextlib.ExitStack() as st:
        S = Sched(nc, st)
        phase_c0(nc, S, cd, w["hy_filt_w1"], w["hy_filt_b1"], w["hy_filt_freq1"], w["hy_filt_w2"], w["hy_filt_b2"], w["hy_filt_freq2"],
                 w["hy_filt_w3"], w["hy_bias"], kt_scr, kspec)
        ffn_phase(nc, S, x, x1, w["ffn1_norm_pre"], w["ffn1_w_gate"], w["ffn1_w_up"], w["ffn1_w_down"], w["ffn1_norm_post"], NT_CORE, T=256)
        phase_b1(nc, S, x1, w["mix_norm_pre"], w["w_in"], hyp, cpre, NSEQ_CORE, conv_tiles=[own[0], own[1] + 1])
        phase_b2(nc, S, hyp, cpre, w["hy_short_w"], w["hy_short_b"], w["cv_dw_w"], w["cv_dw_b"], w["cv_ln_g"], w["cv_ln_b"],
                 vx, x0, cact, NSEQ_CORE, own_tiles=own)
        phase_c1(nc, S, cd, vx, kspec, yspec)
        phase_c2(nc, S, cd, yspec, ylong)
        phase_d(nc, S, x1, x2, w["mix_norm_pre"], w["w_in"], w["hy_w_out"], w["cv_w_out"], w["w_out"], w["mix_norm_post"],
                ylong, x0, cact, [(sq, ti) for sq in range(NSEQ_CORE) for ti in range(own[sq])])
        ffn_phase(nc, S, x2[0:NT_OUT], y, w["ffn2_norm_pre"], w["ffn2_w_gate"], w["ffn2_w_up"], w["ffn2_w_down"], w["ffn2_norm_post"], NT_OUT, T=256)
    return nc


def kernel(**inputs):
    xp = np.asarray(inputs["x_prompt"], dtype=np.float32)
    xs = np.asarray(inputs["x_sample"], dtype=np.float32)
    nown, nsh = xp.shape[0], xs.shape[0]
    consts = host_consts()
    base = {n: np.ascontiguousarray(np.asarray(inputs[n], dtype=np.float32)[0]) for n, _ in W_SPECS}
    rev = dict(base)
    rev["hy_short_w"] = np.ascontiguousarray(base["hy_short_w"][::-1])
    rev["cv_dw_w"] = np.ascontiguousarray(base["cv_dw_w"][::-1])
    w3 = base["hy_filt_w3"]
    rev["hy_filt_w3"] = np.ascontiguousarray(np.concatenate([w3[:, 512:], w3[:, :512]], axis=1))
    for k, v in consts.items():
        base["c_" + k] = v
        rev["c_" + k] = v
    in_maps = []
    for c in range(NCORES):
        odd = c % 2 == 1
        a, b = xp[c], xs[c // 2]
        if odd:
            a, b = a[::-1], b[::-1]
        m = dict(rev if odd else base)
        m["x"] = np.ascontiguousarray(np.concatenate([a, b], axis=0))
        in_maps.append(m)
    nc = build_program()
    res = run_bass_kernel_spmd(nc, in_maps, core_ids=list(range(NCORES)))
    y_prompt = np.empty((nown, L, D), np.float32)
    y_sample = np.empty((nsh, L, D), np.float32)
    H = L // 2
    for c in range(NCORES):
        yc = res.results[c]["y"]
        if c % 2 == 0:
            y_prompt[c] = yc[:L]
            y_sample[c // 2][:H] = yc[L:]
        else:
            y_prompt[c] = yc[:L][::-1]
            y_sample[c // 2][H:] = yc[L:][::-1]
    return (y_prompt, y_sample)
```
